# Optimizing a Trainium2 kernel written in Bass

```python
import math
import jax, jax.numpy as jnp
from jax import lax
import numpy as np

D_MODEL = 1024
BATCH = 4
SEQ = 4096
DEPTH = 1

SB_HEADS = 8
SB_HEAD_DIM = 64
SB_BLOCK = 128
GLA_HEADS = 4
GLA_KEY_DIM = 64
GLA_VAL_DIM = 128
GLA_GATE_RANK = 16
GLA_GATE_TEMP = 16.0
GLA_CHUNK = 64
D_FF = 2816
CONV_WIDTH = 3
N_MOD = 6
DEEPNORM_ALPHA = (2 * DEPTH) ** 0.25
DEEPNORM_BETA = (8 * DEPTH) ** -0.25
LN_EPS = 1e-5
RMS_EPS = 1e-6

SB_WIDTH = SB_HEADS * SB_HEAD_DIM
GLA_QK_WIDTH = GLA_HEADS * GLA_KEY_DIM
GLA_V_WIDTH = GLA_HEADS * GLA_VAL_DIM
MIX_WIDTH = SB_WIDTH + GLA_V_WIDTH
IN_SPLITS = (SB_WIDTH, SB_WIDTH, SB_WIDTH, GLA_QK_WIDTH, GLA_QK_WIDTH, GLA_V_WIDTH, GLA_GATE_RANK, GLA_V_WIDTH)
IN_WIDTH = 3 * SB_WIDTH + 2 * GLA_QK_WIDTH + 2 * GLA_V_WIDTH + GLA_GATE_RANK

kernel_name = "hymba_sb_gla_deepnorm_adaln"


def layer_norm(x, g, b):
    xf = x.astype(jnp.float32)
    mu = jnp.mean(xf, axis=-1, keepdims=True)
    xc = xf - mu
    var = jnp.mean(xc * xc, axis=-1, keepdims=True)
    return (xc * lax.rsqrt(var + LN_EPS) * g.astype(jnp.float32) + b.astype(jnp.float32)).astype(x.dtype)


def head_rms_norm(o):
    of = o.astype(jnp.float32)
    return of * lax.rsqrt(jnp.mean(of * of, axis=-1, keepdims=True) + RMS_EPS)


def stick_breaking_attention(q, k, v):
    B, H, S, d = q.shape
    nb = S // SB_BLOCK
    scale = d ** -0.5
    kf = k.astype(jnp.float32)
    vf = v.astype(jnp.float32)
    qb = q.astype(jnp.float32).reshape(B, H, nb, SB_BLOCK, d).transpose(2, 0, 1, 3, 4)
    key_pos = jnp.arange(S)

    def one_block(args):
        qi, i = args
        z = jnp.einsum('bhqd,bhkd->bhqk', qi, kf) * scale
        q_pos = i * SB_BLOCK + jnp.arange(SB_BLOCK)
        causal = key_pos[None, :] < q_pos[:, None]
        log_beta = jax.nn.log_sigmoid(z)
        log_one_minus = jnp.where(causal, jax.nn.log_sigmoid(-z), 0.0)
        tail = lax.cumsum(log_one_minus, axis=3, reverse=True) - log_one_minus
        w = jnp.where(causal, jnp.exp(log_beta + tail), 0.0)
        return jnp.einsum('bhqk,bhkd->bhqd', w, vf)

    out = lax.map(one_block, (qb, jnp.arange(nb)))
    return out.transpose(1, 2, 0, 3, 4).reshape(B, H, S, d)


def gla_chunked(q, k, v, log_a):
    B, H, S, dk = q.shape
    dv = v.shape[-1]
    C = GLA_CHUNK
    N = S // C
    qf = q.astype(jnp.float32).reshape(B, H, N, C, dk) * dk ** -0.5
    kf = k.astype(jnp.float32).reshape(B, H, N, C, dk)
    vf = v.astype(jnp.float32).reshape(B, H, N, C, dv)
    b = jnp.cumsum(log_a.astype(jnp.float32).reshape(B, H, N, C, dk), axis=3)
    b_last = b[:, :, :, -1:, :]
    q_dec = qf * jnp.exp(b)
    k_inv = kf * jnp.exp(-b)
    k_end = kf * jnp.exp(b_last - b)
    tril = jnp.tril(jnp.ones((C, C), dtype=bool))
    a_intra = jnp.where(tril, jnp.einsum('bhnqd,bhnkd->bhnqk', q_dec, k_inv), 0.0)
    o_intra = jnp.einsum('bhnqk,bhnkd->bhnqd', a_intra, vf)
    kv_chunk = jnp.einsum('bhnkd,bhnke->bhnde', k_end, vf)
    decay = jnp.exp(b_last[:, :, :, 0, :])

    def step(state, inp):
        dec, kv = inp
        return dec[..., None] * state + kv, state

    init = jnp.zeros((B, H, dk, dv), jnp.float32)
    _, states = lax.scan(step, init, (decay.transpose(2, 0, 1, 3), kv_chunk.transpose(2, 0, 1, 3, 4)))
    states = states.transpose(1, 2, 0, 3, 4)
    o_inter = jnp.einsum('bhnqd,bhnde->bhnqe', q_dec, states)
    return (o_intra + o_inter).reshape(B, H, S, dv)


def hybrid_mixer(h, w_in, gla_w_gate, gla_b_gate, sb_norm_g, gla_norm_g, w_out):
    B, S, _ = h.shape
    proj = h @ w_in
    split_at = [int(i) for i in np.cumsum(IN_SPLITS)[:-1]]
    sb_q, sb_k, sb_v, g_q, g_k, g_v, g_lr, g_r = jnp.split(proj, split_at, axis=-1)

    def heads(t, n):
        return t.reshape(B, S, n, -1).transpose(0, 2, 1, 3)

    o_sb = stick_breaking_attention(heads(sb_q, SB_HEADS), heads(sb_k, SB_HEADS), heads(sb_v, SB_HEADS))
    o_sb = head_rms_norm(o_sb.transpose(0, 2, 1, 3)).reshape(B, S, SB_WIDTH) * sb_norm_g.astype(jnp.float32)

    log_a = jax.nn.log_sigmoid((g_lr @ gla_w_gate + gla_b_gate).astype(jnp.float32)) / GLA_GATE_TEMP
    o_gla = gla_chunked(heads(g_q, GLA_HEADS), heads(g_k, GLA_HEADS), heads(g_v, GLA_HEADS), heads(log_a, GLA_HEADS))
    o_gla = head_rms_norm(o_gla.transpose(0, 2, 1, 3)).reshape(B, S, GLA_V_WIDTH) * gla_norm_g.astype(jnp.float32)
    o_gla = o_gla * jax.nn.silu(g_r.astype(jnp.float32))

    o = jnp.concatenate([o_sb, o_gla], axis=-1).astype(h.dtype)
    return o @ w_out


def conv_glu_ffn(h, w_ff_gate, w_ff_up, conv_w, conv_b, w_down):
    S = h.shape[1]
    gate = h @ w_ff_gate
    up = h @ w_ff_up
    gp = jnp.pad(gate, ((0, 0), (CONV_WIDTH - 1, 0), (0, 0)))
    conv = sum(gp[:, i:i + S, :] * conv_w[i] for i in range(CONV_WIDTH)) + conv_b
    return (jax.nn.silu(conv) * up) @ w_down


def setup_inputs(seed: int = 0) -> dict:
    key = jax.random.key(seed)
    ks = jax.random.split(key, 20)
    n = jax.random.normal
    f32 = jnp.float32
    return {
        "x": n(ks[0], (BATCH, SEQ, D_MODEL), f32),
        "c": n(ks[1], (BATCH, D_MODEL), f32),
        "w_ada": n(ks[2], (DEPTH, D_MODEL, N_MOD * D_MODEL), f32) * (0.5 * D_MODEL ** -0.5),
        "b_ada": 0.01 * n(ks[3], (DEPTH, N_MOD * D_MODEL), f32),
        "w_in": n(ks[4], (DEPTH, D_MODEL, IN_WIDTH), f32) * D_MODEL ** -0.5,
        "gla_w_gate": n(ks[5], (DEPTH, GLA_GATE_RANK, GLA_QK_WIDTH), f32) * GLA_GATE_RANK ** -0.5,
        "gla_b_gate": 0.1 * n(ks[6], (DEPTH, GLA_QK_WIDTH), f32),
        "sb_norm_g": 1.0 + 0.02 * n(ks[7], (DEPTH, SB_WIDTH), f32),
        "gla_norm_g": 1.0 + 0.02 * n(ks[8], (DEPTH, GLA_V_WIDTH), f32),
        "w_out": n(ks[9], (DEPTH, MIX_WIDTH, D_MODEL), f32) * (MIX_WIDTH ** -0.5 * DEEPNORM_BETA),
        "ln1_g": 1.0 + 0.02 * n(ks[10], (DEPTH, D_MODEL), f32),
        "ln1_b": 0.01 * n(ks[11], (DEPTH, D_MODEL), f32),
        "w_ff_gate": n(ks[12], (DEPTH, D_MODEL, D_FF), f32) * D_MODEL ** -0.5,
        "w_ff_up": n(ks[13], (DEPTH, D_MODEL, D_FF), f32) * D_MODEL ** -0.5,
        "conv_w": n(ks[14], (DEPTH, CONV_WIDTH, D_FF), f32) * CONV_WIDTH ** -0.5,
        "conv_b": 0.01 * n(ks[15], (DEPTH, D_FF), f32),
        "w_down": n(ks[16], (DEPTH, D_FF, D_MODEL), f32) * (D_FF ** -0.5 * DEEPNORM_BETA),
        "ln2_g": 1.0 + 0.02 * n(ks[17], (DEPTH, D_MODEL), f32),
        "ln2_b": 0.01 * n(ks[18], (DEPTH, D_MODEL), f32),
    }


def reference(x, c, w_ada, b_ada, w_in, gla_w_gate, gla_b_gate, sb_norm_g, gla_norm_g, w_out,
              ln1_g, ln1_b, w_ff_gate, w_ff_up, conv_w, conv_b, w_down, ln2_g, ln2_b):
    c_act = jax.nn.silu(c)
    for l in range(DEPTH):
        mod = c_act @ w_ada[l] + b_ada[l]
        shift_a, scale_a, gate_a, shift_f, scale_f, gate_f = jnp.split(mod[:, None, :], N_MOD, axis=-1)
        h = x * (1.0 + scale_a) + shift_a
        y = hybrid_mixer(h, w_in[l], gla_w_gate[l], gla_b_gate[l], sb_norm_g[l], gla_norm_g[l], w_out[l])
        x = layer_norm(DEEPNORM_ALPHA * x + (1.0 + gate_a) * y, ln1_g[l], ln1_b[l])
        h = x * (1.0 + scale_f) + shift_f
        y = conv_glu_ffn(h, w_ff_gate[l], w_ff_up[l], conv_w[l], conv_b[l], w_down[l])
        x = layer_norm(DEEPNORM_ALPHA * x + (1.0 + gate_f) * y, ln2_g[l], ln2_b[l])
    return x
```

```python
import numpy as np
import ml_dtypes
import concourse.bass as bass
import concourse.mybir as mybir
from concourse.bass_utils import run_bass_kernel_spmd

F32 = mybir.dt.float32
BF16 = mybir.dt.bfloat16
AF = mybir.ActivationFunctionType
ALU = mybir.AluOpType
AX = mybir.AxisListType

P = 128
D = 1024
SEQ = 4096
NBLK = 32
DFF = 2816
NFC = 22
INW = 3088
ALPHA = float(2.0 ** 0.25)
LN_EPS = 1e-5
RMS_EPS = 1e-6
HALO_BLK = 15
NEG = -30000.0

C_ID, C_NEGU, C_MASKNEG, C_TRIS, C_SUPS, C_MASK01, C_ONES = 0, 128, 256, 384, 512, 640, 768
NCF = 896


def cc_col0(cc):
    return 128 * cc if cc <= 20 else 2576 + 128 * (cc - 21)


class Sched:
    def __init__(self, nc, same_engine_sync=True):
        self.nc = nc
        self.same = same_engine_sync
        self.eng = {}
        for name, h in (("pe", nc.tensor), ("act", nc.scalar), ("dve", nc.vector),
                        ("pool", nc.gpsimd), ("sp", nc.sync)):
            self.eng[name] = dict(h=h, sem=nc.alloc_semaphore("s_" + name), count=0, insts=[], waited={})
        self.bufs = {}
        self.dring = {}
        for q, k in (("sp", 16), ("pool", 8), ("act", 4)):
            self.dring[q] = dict(sems=[dict(sem=nc.alloc_semaphore(f"d_{q}{i}"), count=0) for i in range(k)], i=0)
        self.sems = {}
        for e in self.eng.values():
            self.sems[e["sem"].num] = e["sem"]
        for r in self.dring.values():
            for s in r["sems"]:
                self.sems[s["sem"].num] = s["sem"]

    def _deps(self, ename, reads, writes):
        deps = {}

        def add(tok):
            if tok is None:
                return
            n, v = tok
            if deps.get(n, 0) < v:
                deps[n] = v

        for r in reads:
            b = self.bufs.get(r)
            if b is not None:
                add(b[0])
        for w in writes:
            b = self.bufs.get(w)
            if b is not None:
                add(b[0])
                for n, v in b[1].items():
                    add((n, v))
        e = self.eng[ename]
        out = []
        for n, v in deps.items():
            if n == e["sem"].num and (ename == "pe" or not self.same):
                continue
            if e["waited"].get(n, 0) >= v:
                continue
            e["waited"][n] = v
            out.append((n, v))
        return out

    def _commit(self, tok, reads, writes):
        for w in writes:
            self.bufs[w] = [tok, {}]
        for r in reads:
            b = self.bufs.setdefault(r, [None, {}])
            if b[1].get(tok[0], 0) < tok[1]:
                b[1][tok[0]] = tok[1]

    def op(self, ename, fn, reads=(), writes=()):
        e = self.eng[ename]
        waits = self._deps(ename, reads, writes)
        e["count"] += 1
        tok = (e["sem"].num, e["count"])
        e["insts"].append((waits, fn, (e["sem"].num, 1)))
        self._commit(tok, reads, writes)
        return tok

    def dma(self, q, fn, reads=(), writes=(), after=()):
        e = self.eng[q]
        ring = self.dring[q]
        ent = ring["sems"][ring["i"] % len(ring["sems"])]
        ring["i"] += 1
        waits = self._deps(q, reads, writes)
        for (tn, tv) in after:
            if e["waited"].get(tn, 0) < tv:
                e["waited"][tn] = tv
                waits.append((tn, tv))
        n = ent["sem"].num
        if ent["count"] > 0 and e["waited"].get(n, 0) < ent["count"]:
            e["waited"][n] = ent["count"]
            waits.append((n, ent["count"]))
        ent["count"] += 16
        tok = (n, ent["count"])
        e["insts"].append((waits, fn, (n, 16)))
        self._commit(tok, reads, writes)
        return tok

    def barrier(self):
        targets = {}
        for e in self.eng.values():
            if e["count"] > 0:
                targets[e["sem"].num] = e["count"]
        for r in self.dring.values():
            for s in r["sems"]:
                if s["count"] > 0:
                    targets[s["sem"].num] = s["count"]
        for e in self.eng.values():
            waits = []
            for n, v in targets.items():
                if n == e["sem"].num:
                    continue
                if e["waited"].get(n, 0) >= v:
                    continue
                e["waited"][n] = v
                waits.append((n, v))
            if waits:
                e["insts"].append((waits, None, None))

    def replay(self, ename, h):
        for waits, fn, inc in self.eng[ename]["insts"]:
            for n, v in waits:
                h.wait_ge(self.sems[n], v)
            if fn is not None:
                ins = fn(h)
                ins.then_inc(self.sems[inc[0]], inc[1])


def build_program(debug=None):
    nc = bass.Bass("TRN2", target_bir_lowering=False)
    S = Sched(nc)

    def din(name, shape, dt=F32):
        return nc.dram_tensor(name, list(shape), dt, kind="ExternalInput").ap()

    xw = din("xw", [SEQ, D])
    cf_d = din("cf", [P, NCF])
    small_d = din("small", [P, 32 + 1 + 8 + 8 + 88])
    w_ada = din("w_ada", [D, 6 * D])
    b_ada = din("b_ada", [1, 6 * D])
    w_in = din("w_in", [D, INW])
    gwg = din("gla_w_gate", [16, 256])
    gbg = din("gla_b_gate", [1, 256])
    w_out = din("w_out", [D, D])
    ln1_g = din("ln1_g", [1, D])
    ln1_b = din("ln1_b", [1, D])
    w_gate = din("w_ff_gate", [D, DFF])
    w_up = din("w_ff_up", [D, DFF])
    w_down = din("w_down", [DFF, D])
    ln2_g = din("ln2_g", [1, D])
    ln2_b = din("ln2_b", [1, D])
    out_d = nc.dram_tensor("out", [2048, D], F32, kind="ExternalOutput").ap()

    WIN_S = nc.dram_tensor("win_s", [25, P, 8, 128], BF16).ap()
    WG_S = nc.dram_tensor("wg_s", [NFC, P, 8, 128], BF16).ap()
    WU_S = nc.dram_tensor("wu_s", [NFC, P, 8, 128], BF16).ap()
    WD_S = nc.dram_tensor("wd_s", [NFC, P, D], BF16).ap()
    WOUT_S = nc.dram_tensor("wout_s", [P, 8, D], BF16).ap()

    def sb(name, shape, dt):
        return nc.alloc_sbuf_tensor(name, list(shape), dt)

    KT = sb("KT", [P, 4, SEQ], BF16)
    VA = sb("VA", [P, NBLK, 8, 65], BF16)
    CF = sb("CF", [P, NCF], F32)
    SMALL = sb("SMALL", [P, 137], F32)
    TM = SMALL[:, 0:32]
    HM = SMALL[:, 32:33]
    CCOL = SMALL[:, 33:41]
    GCAT = SMALL[:, 41:49]
    CW = SMALL[:, 49:137].rearrange("p (f k) -> p f k", k=4)
    IDB = sb("IDB", [P, P], BF16)
    NEGUB = sb("NEGUB", [P, P], BF16)
    MASKNEGB = sb("MASKNEGB", [P, P], BF16)
    NEGONESB = sb("NEGONESB", [P, P], BF16)
    MODP = sb("MODP", [P, 48], F32)
    CACT = sb("CACT", [P, 8], F32)
    LN2G = sb("LN2G", [P, D], F32)
    LN2B = sb("LN2B", [P, D], F32)
    S2 = sb("S2", [P, 2, 128], F32)
    S2B = sb("S2B", [P, 2, 128], BF16)
    WGA = sb("WGA", [33, 256], F32)
    LRA = sb("LRA", [33, 512], F32)
    HALO = sb("HALO", [P, NFC, 2], F32)
    STAT = sb("STAT", [P, 64], F32)

    ARENA_BYTES = 125 * 1024
    ARENA = sb("ARENA", [P, ARENA_BYTES // 4], F32)

    class Carver:
        def __init__(self, base=0):
            self.off = base

        def take(self, shape, dt, parts=P):
            esz = 4 if dt == F32 else 2
            n = int(np.prod(shape[1:]))
            nbytes = (n * esz + 31) // 32 * 32
            assert self.off % 4 == 0
            assert self.off + nbytes <= ARENA_BYTES, ("arena overflow", self.off, nbytes)
            w0 = self.off // 4
            ap = ARENA[0:shape[0], w0:w0 + nbytes // 4]
            if dt != F32:
                ap = ap.bitcast(dt)
            ap = ap[:, 0:n]
            self.off += nbytes
            if len(shape) == 3:
                ap = ap.rearrange("p (a b) -> p a b", a=shape[1])
            elif len(shape) == 4:
                ap = ap.rearrange("p (a b c) -> p a b c", a=shape[1], b=shape[2])
            return ap

    PP = [nc.alloc_psum_tensor(f"PP{i}", [P, 1024], F32) for i in range(4)]
    PB = [PP[i // 2][:, (i % 2) * 512:(i % 2 + 1) * 512] for i in range(8)]
    PT = PB[7].bitcast(BF16)
    PTK = ("PB", 7)
    PZ2, PC2 = PP[0], PP[1]
    pools = {"gen": [0, 1, 2, 3, 4, 5, 6], "po": [4, 5], "ffn": [6, 7], "all": [0, 1, 2, 3, 4, 5, 6, 7]}
    pbi = {"gen": 0, "po": 0, "ffn": 0, "all": 0}

    def nb(pool="gen"):
        lst = pools[pool]
        i = lst[pbi[pool] % len(lst)]
        pbi[pool] += 1
        return PB[i], ("PB", i)

    def mm(out, lhsT, rhs, start, stop, reads, writes):
        return S.op("pe", lambda e: e.matmul(out, lhsT=lhsT, rhs=rhs, start=start, stop=stop), reads, writes)

    def tr(out, in_, ident, reads, writes):
        return S.op("pe", lambda e: e.transpose(out=out, in_=in_, identity=ident), reads, writes)

    def act(out, in_, func, reads, writes, bias=0.0, scale=1.0):
        return S.op("act", lambda e: e.activation(out=out, in_=in_, func=func, bias=bias, scale=scale), reads, writes)

    def vts(out, in0, s1, s2, op0, op1, reads, writes, eng="dve"):
        return S.op(eng, lambda e: e.tensor_scalar(out=out, in0=in0, scalar1=s1, scalar2=s2, op0=op0, op1=op1),
                    reads, writes)

    def vtt(out, in0, in1, op, reads, writes, eng="dve"):
        return S.op(eng, lambda e: e.tensor_tensor(out=out, in0=in0, in1=in1, op=op), reads, writes)

    def vstt(out, in0, scalar, in1, op0, op1, reads, writes, eng="dve"):
        return S.op(eng, lambda e: e.scalar_tensor_tensor(out=out, in0=in0, scalar=scalar, in1=in1, op0=op0, op1=op1),
                    reads, writes)

    def vcopy(out, in_, reads, writes, eng="dve"):
        return S.op(eng, lambda e: e.tensor_copy(out=out, in_=in_), reads, writes)

    def vmemset(ap, val, writes, eng="dve"):
        return S.op(eng, lambda e: e.memset(ap, val), (), writes)

    def ld(out, in_, reads, writes, q="sp", after=()):
        return S.dma(q, lambda e: e.dma_start(out=out, in_=in_), reads, writes, after)

    ld(CF[:], cf_d, (), ["CF"])
    ld(SMALL[:], small_d, (), ["SMALL"])
    vcopy(IDB[:], CF[:, C_ID:C_ID + 128], ["CF"], ["IDB"])
    vcopy(NEGUB[:], CF[:, C_NEGU:C_NEGU + 128], ["CF"], ["NEGUB"])
    vcopy(MASKNEGB[:], CF[:, C_MASKNEG:C_MASKNEG + 128], ["CF"], ["MASKNEGB"])
    vts(NEGONESB[:], CF[:, C_ONES:C_ONES + 128], -1.0, None, ALU.mult, ALU.bypass, ["CF"], ["NEGONESB"])
    IDF = CF[:, C_ID:C_ID + 128]
    TRIS = CF[:, C_TRIS:C_TRIS + 128]
    SUPS = CF[:, C_SUPS:C_SUPS + 128]
    MASK01 = CF[:, C_MASK01:C_MASK01 + 128]
    ONE11 = CF[0:1, 0:1]
    ONESROW = CF[0:1, C_ONES:C_ONES + 128]

    def cast_win(cc, after=()):
        c0 = cc_col0(cc)
        ncol = 16 if cc == 20 else 128
        src = w_in[:, c0:c0 + ncol].rearrange("(kc p) n -> p kc n", p=P)
        ld(WIN_S[cc, :, :, 0:ncol], src, (), [("WIN_S", cc)], q="pool", after=after)

    Q_CCS = [0, 1, 2, 3, 12, 13, 21, 22, 23, 24]
    for cc in range(25):
        if cc not in Q_CCS:
            cast_win(cc)

    deferred = []
    for cc in Q_CCS:
        deferred.append(lambda after, cc=cc: cast_win(cc, after))
    ld(LN2G[:], ln2_g.partition_broadcast(P), (), ["LN2G"])
    ld(LN2B[:], ln2_b.partition_broadcast(P), (), ["LN2B"])
    vmemset(WGA[:], 0.0, ["WGA"])
    ld(WGA[0:16, :], gwg, (), ["WGA"])
    ld(WGA[32:33, :], gbg, (), ["WGA"])
    vmemset(LRA[:], 0.0, ["LRA"])
    vmemset(LRA[32:33, :], 1.0, ["LRA"])
    vmemset(S2[:], 0.0, ["S2"])
    vmemset(S2B[:], 0.0, ["S2B"])
    vmemset(HALO[:], 0.0, ["HALO"])
    vcopy(VA[:, :, :, 64], TM.unsqueeze(2).to_broadcast([P, NBLK, 8]), ["SMALL"], ["VA1"])

    cv = Carver(0)
    G2B = cv.take([P, D], F32)
    WO32 = [cv.take([P, D], F32) for _ in range(2)]
    WOB = [cv.take([P, D], BF16) for _ in range(2)]
    assert cv.off <= 24 * 1024
    cv.off = 24 * 1024
    WA = [cv.take([P, 8, 512], F32) for _ in range(2)]
    MODROW = cv.take([1, 6 * D], F32)
    BADA = cv.take([1, 6 * D], F32)
    G1B = cv.take([P, D], F32)
    act(CACT[:], CCOL, AF.Silu, ["SMALL"], ["CACT"])
    ld(BADA, b_ada, (), ["BADA"])
    for pc in range(12):
        s = pc % 2
        src = w_ada[:, pc * 512:(pc + 1) * 512].rearrange("(kc p) n -> p kc n", p=P)
        ld(WA[s], src, (), [("WA", s)])
        bank, bk = nb()
        for kc in range(8):
            mm(bank[0:1, :], CACT[:, kc:kc + 1], WA[s][:, kc, :], kc == 0, False, [("WA", s), "CACT"], [bk])
        mm(bank[0:1, :], ONE11, BADA[0:1, pc * 512:(pc + 1) * 512], False, True, ["BADA", "CF"], [bk])
        vcopy(MODROW[0:1, pc * 512:(pc + 1) * 512], bank[0:1, :], [bk], ["MODROW"])
    bank, bk = nb()
    for j in range(48):
        mm(bank[:, j:j + 1], MODROW[0:1, j * 128:(j + 1) * 128], ONE11, True, True, ["MODROW", "CF"], [bk])
    vcopy(MODP[:], bank[:, 0:48], [bk], ["MODP"])
    vts(MODP[:, 8:16], MODP[:, 8:16], 1.0, None, ALU.add, ALU.bypass, ["MODP"], ["MODP"])
    vts(MODP[:, 32:40], MODP[:, 32:40], 1.0, None, ALU.add, ALU.bypass, ["MODP"], ["MODP"])
    for (dst, dk, base) in ((G1B, "G1B", 2 * D), (G2B, "G2B", 5 * D)):
        for hf in range(2):
            bank, bk = nb()
            mm(bank[:, :], ONESROW, MODROW[0:1, base + hf * 512: base + (hf + 1) * 512], True, True,
               ["MODROW", "CF"], [bk])
            vts(dst[:, hf * 512:(hf + 1) * 512], bank[:, :], 1.0, None, ALU.add, ALU.bypass, [bk], [dk])
    for kc in range(8):
        s = kc % 2
        ld(WO32[s], w_out[kc * 128:(kc + 1) * 128, :], (), [("WO32", s)])
        vstt(WOB[s], WO32[s], GCAT[:, kc:kc + 1], G1B, ALU.mult, ALU.mult,
             [("WO32", s), "SMALL", "G1B"], [("WOB", s)])
        ld(WOUT_S[:, kc, :], WOB[s], [("WOB", s)], ["WOUT_S"])
    for fc in range(NFC):
        def _wd(after, fc=fc):
            s = fc % 2
            ld(WO32[s], w_down[fc * 128:(fc + 1) * 128, :], (), [("WO32", s)], q="pool", after=after)
            vtt(WOB[s], WO32[s], G2B, ALU.mult, [("WO32", s), "G2B"], [("WOB", s)], eng="pool")
            ld(WD_S[fc], WOB[s], [("WOB", s)], [("WD_S", fc)], q="pool")
        deferred.append(_wd)
    for fc in range(NFC):
        def _cg(after, fc=fc):
            src = w_gate[:, fc * 128:(fc + 1) * 128].rearrange("(kc p) n -> p kc n", p=P)
            ld(WG_S[fc], src, (), [("WG_S", fc)], q="pool", after=after)
        def _cu(after, fc=fc):
            src = w_up[:, fc * 128:(fc + 1) * 128].rearrange("(kc p) n -> p kc n", p=P)
            ld(WU_S[fc], src, (), [("WU_S", fc)], q="pool", after=after)
        deferred.append(_cg)
        deferred.append(_cu)

    S.barrier()

    cv = Carver(0)
    X1 = cv.take([P, 4, D], F32)
    H2T = cv.take([P, 8, 512], BF16)
    QT = cv.take([P, 8, 512], BF16)
    ON = cv.take([P, 4, D], BF16)
    base2 = cv.off
    ES2 = cv.take([P, 2, 512], F32)
    SQ8 = ES2[:, 0, :].rearrange("p (h d) -> p h d", h=8)
    SP2 = [cv.take([P, 2, 512], BF16) for _ in range(2)]
    WT2 = [cv.take([P, 2, 512], BF16) for _ in range(2)]
    OACC = cv.take([P, 4, 8, 64], F32)
    baseEF = cv.off
    WGU = [cv.take([P, 2, 8, 128], BF16) for _ in range(4)]
    GS = [cv.take([P, 520], F32) for _ in range(2)]
    CT = [cv.take([P, 512], F32) for _ in range(2)]
    SL = [cv.take([P, 512], F32) for _ in range(2)]
    US = [cv.take([P, 512], BF16) for _ in range(2)]
    ACTT = cv.take([P, NFC, 512], BF16)
    WDR = [cv.take([P, D], BF16) for _ in range(3)]
    U2 = cv.take([P, D], F32)
    endEF = cv.off
    cv.off = baseEF
    XB2s = [cv.take([P, D], F32) for _ in range(4)]
    OTs = [cv.take([P, 8, 128], BF16) for _ in range(4)]
    Us = [cv.take([P, D], F32) for _ in range(4)]
    DSTAT = cv.take([P, 4, 16], F32)
    WOUTT = cv.take([P, 8, D], BF16)
    LN1G = cv.take([P, D], F32)
    LN1B = cv.take([P, D], F32)
    endD = cv.off
    cv.off = base2
    XBs = [cv.take([P, D], F32) for _ in range(4)]
    HT = cv.take([P, 8, 512], BF16)
    RA = [cv.take([P, 8, 128], BF16) for _ in range(4)]
    RB = [cv.take([P, 2, 8, 128], BF16) for _ in range(3)]
    GQT = cv.take([P, 2, 512], F32)
    GKT = cv.take([P, 2, 512], F32)
    GK = cv.take([P, 4, 256], F32)
    GV = cv.take([P, 4, 512], BF16)
    SR = cv.take([P, 4, 512], F32)
    EU = cv.take([P, 256], F32)
    LL = cv.take([P, 256], F32)
    EB = cv.take([P, 2, 128], F32)
    ENB = cv.take([P, 2, 128], F32)
    ED = cv.take([P, 256], F32)
    QD = cv.take([P, 2, 128], BF16)
    KI = cv.take([P, 2, 128], BF16)
    KE = cv.take([P, 256], BF16)
    AM = cv.take([P, 2, 2, 128], BF16)
    OC = cv.take([P, 2, 2, 128], F32)
    SQ = cv.take([P, 2, 2, 128], F32)
    endAB = cv.off

    ra_i = [0]
    rb_i = [0]
    es_i = [0]

    def phase_A(wt):
        own = wt >= 4
        halo = wt == 3
        for j in range(4):
            blk = 4 * wt + j
            XB = XBs[j]
            xk = ("XB", j)
            ld(XB, xw[blk * 128:(blk + 1) * 128, :], (), [xk])
            for hf in range(2):
                bank, bk = nb()
                for q in range(4):
                    kc = hf * 4 + q
                    tr(bank[:, q * 128:(q + 1) * 128], XB[:, kc * 128:(kc + 1) * 128], IDF, [xk, "CF"], [bk])
                for q in range(4):
                    kc = hf * 4 + q
                    vts(HT[:, kc, j * 128:(j + 1) * 128], bank[:, q * 128:(q + 1) * 128],
                        MODP[:, 8 + kc:9 + kc], MODP[:, kc:kc + 1], ALU.mult, ALU.add,
                        [bk, "MODP"], [("HT", j)])
        htk = [("HT", j) for j in range(4)]

        def fm_job(cc, ncol, c0, c1, evac):
            s = ra_i[0] % 4
            ra_i[0] += 1
            ld(RA[s][:, :, 0:ncol], WIN_S[cc, :, :, 0:ncol], [("WIN_S", cc)], [("RA", s)])
            bank, bk = nb()
            for kc in range(8):
                mm(bank[0:ncol, c0:c1], RA[s][:, kc, 0:ncol], HT[:, kc, c0:c1], kc == 0, kc == 7,
                   [("RA", s)] + htk, [bk])
            evac(bank, bk)

        for gi in range(4):
            fm_job(4 + gi, 128, 0, 512,
                   lambda bank, bk, gi=gi: vcopy(KT[:, gi, wt * 512:(wt + 1) * 512], bank[:, :], [bk], [("KT", gi, wt)]))
        for g2 in range(2):
            fm_job(14 + g2, 128, 0, 512,
                   lambda bank, bk, g2=g2: vcopy(GKT[:, g2, :], bank[:, :], [bk], ["GKT"]))
        fm_job(20, 16, 0, 512, lambda bank, bk: vcopy(LRA[0:16, :], bank[0:16, :], [bk], ["LRA"]))
        if own or halo:
            q0 = 0 if own else 384
            vmemset(QT, 0.0, ["QT"])
            for gi in range(4):
                def evq(bank, bk, gi=gi):
                    for par in range(2):
                        r0 = par * 64
                        vts(QT[r0:r0 + 64, 2 * gi + par, q0:512], bank[r0:r0 + 64, q0:512], 0.125, None, ALU.mult,
                            ALU.bypass, [bk], ["QT"])
                fm_job(gi, 128, q0, 512, evq)
            for g2 in range(2):
                fm_job(12 + g2, 128, q0, 512,
                       lambda bank, bk, g2=g2: vcopy(GQT[:, g2, q0:512], bank[:, q0:512], [bk], ["GQT"]))

        def tm_job(cc, blocks, evac):
            s = rb_i[0] % 3
            rb_i[0] += 1
            ld(RB[s], WIN_S[cc:cc + 2].rearrange("c p k n -> p c k n"), [("WIN_S", cc), ("WIN_S", cc + 1)],
               [("RB", s)])
            for j in blocks:
                bank, bk = nb()
                for kc in range(8):
                    mm(bank[:, 0:256], HT[:, kc, j * 128:(j + 1) * 128], RB[s][:, :, kc, :], kc == 0, kc == 7,
                       [("RB", s), ("HT", j)], [bk])
                evac(bank, bk, j)

        allb = range(4)
        for pi in range(2):
            def ev(bank, bk, j, pi=pi):
                blk = 4 * wt + j
                vts(VA[:, blk, 4 * pi:4 * pi + 4, 0:64], bank[:, 0:256].rearrange("p (h d) -> p h d", h=4),
                    TM[:, blk:blk + 1], None, ALU.mult, ALU.bypass, [bk, "SMALL"], [("VA", blk)])
            tm_job(8 + 2 * pi, allb, ev)
        tm_job(14, allb, lambda bank, bk, j: vcopy(GK[:, j, :], bank[:, 0:256], [bk], [("GK", j)]))
        for pi in range(2):
            def ev(bank, bk, j, pi=pi):
                blk = 4 * wt + j
                vts(GV[:, j, pi * 256:(pi + 1) * 256], bank[:, 0:256], TM[:, blk:blk + 1], None, ALU.mult,
                    ALU.bypass, [bk, "SMALL"], [("GV", j)])
            tm_job(16 + 2 * pi, allb, ev)
        if own or halo:
            qblocks = range(4) if own else [3]
            for pi in range(2):
                def ev(bank, bk, j, pi=pi):
                    act(SR[:, j, pi * 256:(pi + 1) * 256], bank[:, 0:256], AF.Silu, [bk], [("SR", j)])
                tm_job(21 + 2 * pi, qblocks, ev)

    def phase_B(wt):
        for j in range(4):
            blk = 4 * wt + j
            full = blk >= HALO_BLK
            jc = slice(j * 128, (j + 1) * 128)
            bu, bku = nb()
            mm(bu[:, 0:256], LRA[0:33, jc], WGA[0:33, :], True, True, ["LRA", "WGA"], [bku])
            act(EU, bu[:, 0:256], AF.Exp, [bku], ["EU"], scale=-1.0)
            act(LL, EU, AF.Ln, ["EU"], ["LL"], bias=1.0)
            bb, bkb = nb()
            for g2 in range(2):
                mm(bb[:, g2 * 128:(g2 + 1) * 128], LL[:, g2 * 128:(g2 + 1) * 128], TRIS, True, True, ["LL", "CF"], [bkb])
            bd, bkd = nb()
            mm(bd[:, 0:256], SUPS, LL, True, True, ["LL", "CF"], [bkd])
            bb3 = bb[:, 0:256].rearrange("p (g t) -> p g t", g=2)
            if full:
                act(EB, bb3, AF.Exp, [bkb], ["EB"], bias=float(np.log(0.125)))
            act(ENB, bb3, AF.Exp, [bkb], ["ENB"], scale=-1.0)
            act(STAT[:, 0:2], bb3[:, :, 127], AF.Exp, [bkb], ["DEC"])
            act(ED, bd[:, 0:256], AF.Exp, [bkd], ["ED"])
            vtt(KI, GKT[:, :, jc], ENB, ALU.mult, ["GKT", "ENB"], ["KI"])
            vtt(KE, GK[:, j, :], ED, ALU.mult, [("GK", j), "ED"], ["KE"])
            if full:
                jq = j if wt >= 4 else 3
                vtt(QD, GQT[:, :, jc], EB, ALU.mult, ["GQT", "EB"], ["QD"])
                bas = [nb(), nb()]
                for h in range(4):
                    g2, par = h // 2, h % 2
                    r0 = par * 64
                    ba, bka = bas[par]
                    mm(ba[:, g2 * 128:(g2 + 1) * 128], KI[r0:r0 + 64, g2, :], QD[r0:r0 + 64, g2, :], True, True,
                       ["KI", "QD"], [bka])
                for par in range(2):
                    ba, bka = bas[par]
                    vtt(AM[:, par], ba[:, 0:256].rearrange("p (g t) -> p g t", g=2),
                        MASK01.unsqueeze(1).to_broadcast([P, 2, 128]), ALU.mult, [bka, "CF"], [("AM", par)])
                bos = [nb(), nb()]
                for h in range(4):
                    g2, par = h // 2, h % 2
                    r0 = par * 64
                    bo, bko = bos[par]
                    mm(bo[:, g2 * 128:(g2 + 1) * 128], AM[:, par, g2, :], GV[:, j, h * 128:(h + 1) * 128], True, False,
                       [("AM", par), ("GV", j)], [bko])
                    mm(bo[:, g2 * 128:(g2 + 1) * 128], QD[r0:r0 + 64, g2, :], S2B[r0:r0 + 64, g2, :], False, True,
                       ["QD", "S2B"], [bko])
                for par in range(2):
                    bo, bko = bos[par]
                    vcopy(OC[:, par], bo[:, 0:256].rearrange("p (g t) -> p g t", g=2), [bko], ["OC"])
                vtt(SQ, OC, OC, ALU.mult, ["OC"], ["SQ"])
                S.op("dve", lambda e: e.reduce_sum(out=STAT[:, 8:12], in_=SQ.rearrange("p r g t -> p (r g) t"), axis=AX.X),
                     ["SQ"], ["SS4"])
                vts(STAT[:, 8:12], STAT[:, 8:12], 1.0 / 128.0, RMS_EPS, ALU.mult, ALU.add, ["SS4"], ["SS4"])
                act(STAT[:, 8:12], STAT[:, 8:12], AF.Ln, ["SS4"], ["SS4"])
                act(STAT[:, 8:12], STAT[:, 8:12], AF.Exp, ["SS4"], ["SS4"], scale=-0.5)
                OC3 = OC.rearrange("p r g t -> p (r g) t")
                vtt(OC3, OC3, STAT[:, 8:12].unsqueeze(2).to_broadcast([P, 4, 128]), ALU.mult, ["OC", "SS4"], ["OC"])
                for par in range(2):
                    vtt(ON[:, jq, 512:1024].rearrange("p (g r t) -> p r g t", g=2, r=2)[:, par], OC[:, par],
                        SR[:, j, :].rearrange("p (g r t) -> p r g t", g=2, r=2)[:, par], ALU.mult,
                        ["OC", ("SR", j)], [("ON", jq, 1)])
            bk_, bkk = nb()
            for h in range(4):
                g2 = h // 2
                mm(bk_[:, h * 128:(h + 1) * 128], KE[:, g2 * 128:(g2 + 1) * 128], GV[:, j, h * 128:(h + 1) * 128],
                   True, True, ["KE", ("GV", j)], [bkk])
            for h in range(4):
                g2, r0 = h // 2, (h % 2) * 64
                vstt(S2[r0:r0 + 64, g2, :], S2[r0:r0 + 64, g2, :], STAT[r0:r0 + 64, g2:g2 + 1],
                     bk_[r0:r0 + 64, h * 128:(h + 1) * 128], ALU.mult, ALU.add, ["S2", "DEC", bkk], ["S2"])
            tok = vcopy(S2B[:], S2[:], ["S2"], ["S2B"])
            for _ in range(6):
                if deferred:
                    deferred.pop(0)([tok])

    def phase_C(wt):
        own = wt >= 4
        qb_first = 4 * wt if own else HALO_BLK
        qb_last = 4 * wt + 3
        units = []
        for h in range(8):
            kb = 0
            while kb <= qb_last:
                if kb + 1 < qb_first:
                    units.append((h, [kb, kb + 1]))
                    kb += 2
                else:
                    units.append((h, [kb]))
                    kb += 1
        ctx = {}

        def prep(n):
            h, kbs = units[n]
            gi = h // 2
            qa = max(kbs[0], qb_first)
            lc0 = (qa - 4 * wt) * 128
            N = 512 - lc0
            c = dict(h=h, kbs=kbs, gi=gi, qa=qa, lc0=lc0, N=N, nq=N // 128, diag=kbs[0] >= qb_first,
                     kts=[KT[:, gi, kb * 128:(kb + 1) * 128] for kb in kbs], qt=QT[:, h, lc0:512],
                     rds=[[("KT", gi, kb // 4), "QT"] for kb in kbs], s=n % 2, nb=len(kbs))
            ctx[n] = c
            return c

        def st1(n):
            c = prep(n)
            N, nbk = c["N"], c["nb"]
            for bi in range(nbk):
                mm(PZ2[:, bi * 512:bi * 512 + N], c["kts"][bi], c["qt"], True, not c["diag"], c["rds"][bi], [("PB", bi)])
                if c["diag"]:
                    mm(PZ2[:, 0:128], IDB[:], MASKNEGB[:], False, True, ["IDB", "MASKNEGB"], [("PB", bi)])
            pk = [("PB", bi) for bi in range(nbk)]
            pzv = PZ2[:, :].rearrange("p (b n) -> p b n", b=2)[:, 0:nbk, 0:N]
            act(ES2[:, 0:nbk, 0:N], pzv, AF.Exp, pk, ["ES2"])
            act(SP2[c["s"]][:, 0:nbk, 0:N], ES2[:, 0:nbk, 0:N], AF.Ln, ["ES2"], [("SP2", c["s"])], bias=1.0)

        def st2(n):
            c = ctx[n]
            N, nbk, s = c["N"], c["nb"], c["s"]
            for bi in range(nbk):
                bk = ("PB", 2 + bi)
                dst = PC2[:, bi * 512:bi * 512 + N]
                mm(dst, c["kts"][bi], c["qt"], True, False, c["rds"][bi], [bk])
                if c["diag"]:
                    mm(PC2[:, 0:128], IDB[:], MASKNEGB[:], False, False, ["IDB", "MASKNEGB"], [bk])
                last = not (nbk == 2 and bi == 0)
                mm(dst, NEGUB[:], SP2[s][:, bi, 0:N], False, last, ["NEGUB", ("SP2", s)], [bk])
                if nbk == 2 and bi == 0:
                    mm(dst, NEGONESB[:], SP2[s][:, 1, 0:N], False, True, ["NEGONESB", ("SP2", s)], [bk])
            pk = [("PB", 2 + bi) for bi in range(nbk)]
            pcv = PC2[:, :].rearrange("p (b n) -> p b n", b=2)[:, 0:nbk, 0:N]
            act(WT2[s][:, 0:nbk, 0:N], pcv, AF.Exp, pk, [("WT2", s)])

        def st3(n):
            c = ctx.pop(n)
            h, kbs, nq, s, nbk = c["h"], c["kbs"], c["nq"], c["s"], c["nb"]
            po, bo = nb("po")
            for qi in range(nq):
                for bi in range(nbk):
                    mm(po[:, qi * 65:(qi + 1) * 65], WT2[s][:, bi, qi * 128:(qi + 1) * 128], VA[:, kbs[bi], h, :],
                       bi == 0, bi == nbk - 1, [("WT2", s), ("VA", kbs[bi]), "VA1"], [bo])
            jq0 = c["qa"] - 4 * wt
            po3 = po[:, 0:nq * 65].rearrange("p (q e) -> p q e", e=65)
            if kbs[0] == 0:
                vcopy(OACC[:, jq0:jq0 + nq, h, :], po3[:, :, 0:64], [bo], [("OACC", h)])
            else:
                F = STAT[:, 16:16 + nq]
                vts(F, po3[:, :, 64], -1.0, 1.0, ALU.mult, ALU.add, [bo], ["F"])
                if nq == 1:
                    vstt(OACC[:, jq0, h, :], OACC[:, jq0, h, :], STAT[:, 16:17],
                         po3[:, 0, 0:64], ALU.mult, ALU.add, [("OACC", h), "F", bo], [("OACC", h)])
                else:
                    oa = OACC[:, jq0:jq0 + nq, h, :]
                    vtt(oa, oa, F.unsqueeze(2).to_broadcast([P, nq, 64]), ALU.mult, [("OACC", h), "F"], [("OACC", h)])
                    vtt(oa, oa, po3[:, :, 0:64], ALU.add, [("OACC", h), bo], [("OACC", h)])

        nblk = len(units)
        for t in range(nblk + 2):
            if t < nblk:
                st1(t)
            if 0 <= t - 1 < nblk:
                st2(t - 1)
            if 0 <= t - 2 < nblk:
                st3(t - 2)
            yield
        jqs = range(4) if own else [3]
        for jq in jqs:
            vtt(SQ8, OACC[:, jq], OACC[:, jq], ALU.mult, [("OACC", h) for h in range(8)], ["ES2"])
            S.op("dve", lambda e: e.reduce_sum(out=STAT[:, 24:32], in_=SQ8, axis=AX.X), ["ES2"], ["SS8"])
            vts(STAT[:, 24:32], STAT[:, 24:32], 1.0 / 64.0, RMS_EPS, ALU.mult, ALU.add, ["SS8"], ["SS8"])
            act(STAT[:, 24:32], STAT[:, 24:32], AF.Ln, ["SS8"], ["SS8"])
            act(STAT[:, 24:32], STAT[:, 24:32], AF.Exp, ["SS8"], ["SS8"], scale=-0.5)
            vtt(ON[:, jq, 0:512].rearrange("p (h d) -> p h d", h=8), OACC[:, jq],
                STAT[:, 24:32].unsqueeze(2).to_broadcast([P, 8, 64]), ALU.mult,
                [("OACC", h) for h in range(8)] + ["SS8"], [("ON", jq, 0)])

    def layer_norm(Ut, uk, G, gk, Bt, bkey, out, okey):
        for hf in range(2):
            S.op("dve", lambda e, hf=hf: e.bn_stats(out=STAT[:, 32 + 6 * hf:38 + 6 * hf], in_=Ut[:, hf * 512:(hf + 1) * 512]),
                 [uk], [("BN", hf)])
        S.op("dve", lambda e: e.bn_aggr(out=STAT[:, 44:46], in_=STAT[:, 32:44]), [("BN", 0), ("BN", 1)], ["MV"])
        act(STAT[:, 46:47], STAT[:, 45:46], AF.Ln, ["MV"], ["RSTD"], bias=LN_EPS)
        act(STAT[:, 46:47], STAT[:, 46:47], AF.Exp, ["RSTD"], ["RSTD"], scale=-0.5)
        vts(STAT[:, 47:48], STAT[:, 44:45], STAT[:, 46:47], -1.0, ALU.mult, ALU.mult, ["MV", "RSTD"], ["NMR"])
        vts(Ut, Ut, STAT[:, 46:47], STAT[:, 47:48], ALU.mult, ALU.add, [uk, "RSTD", "NMR"], [uk])
        vtt(Ut, Ut, G, ALU.mult, [uk, gk], [uk])
        vtt(out, Ut, Bt, ALU.add, [uk, bkey], [okey])

    def phase_D(wt):
        own = wt >= 4
        ld(WOUTT, WOUT_S, ["WOUT_S"], ["WOUTT"])
        ld(LN1G, ln1_g.partition_broadcast(P), (), ["LN1G"])
        ld(LN1B, ln1_b.partition_broadcast(P), (), ["LN1B"])
        jqs = list(range(4)) if own else [3]
        for jq in jqs:
            blk = 4 * wt + jq
            ld(XB2s[jq], xw[blk * 128:(blk + 1) * 128, :], (), [("XB2", jq)])
        for jq in jqs:
            for kc in range(8):
                tr(PT[:, kc * 128:(kc + 1) * 128], ON[:, jq, kc * 128:(kc + 1) * 128], IDB[:],
                   [("ON", jq, 0), ("ON", jq, 1), "IDB"], [PTK])
            vcopy(OTs[jq], PT[:, :].rearrange("p (k t) -> p k t", k=8), [PTK], [("OT", jq)])
        for jq in jqs:
            for hf in range(2):
                py, bky = nb()
                for kc in range(8):
                    mm(py[:, :], OTs[jq][:, kc, :], WOUTT[:, kc, hf * 512:(hf + 1) * 512], kc == 0, kc == 7,
                       [("OT", jq), "WOUTT"], [bky])
                vstt(Us[jq][:, hf * 512:(hf + 1) * 512], XB2s[jq][:, hf * 512:(hf + 1) * 512], ALPHA, py[:, :],
                     ALU.mult, ALU.add, [("XB2", jq), bky], [("U", jq)])
        for jq in jqs:
            for hf in range(2):
                S.op("dve", lambda e, hf=hf, jq=jq: e.bn_stats(out=DSTAT[:, jq, 6 * hf:6 * hf + 6],
                                                              in_=Us[jq][:, hf * 512:(hf + 1) * 512]),
                     [("U", jq)], [("DST", jq)])
            S.op("dve", lambda e, jq=jq: e.bn_aggr(out=DSTAT[:, jq, 12:14], in_=DSTAT[:, jq, 0:12]),
                 [("DST", jq)], [("DST", jq)])
        for jq in jqs:
            act(DSTAT[:, jq, 14:15], DSTAT[:, jq, 13:14], AF.Ln, [("DST", jq)], [("DRS", jq)], bias=LN_EPS)
        for jq in jqs:
            act(DSTAT[:, jq, 14:15], DSTAT[:, jq, 14:15], AF.Exp, [("DRS", jq)], [("DRS", jq)], scale=-0.5)
        for jq in jqs:
            U = Us[jq]
            uk = ("U", jq)
            vts(DSTAT[:, jq, 15:16], DSTAT[:, jq, 12:13], DSTAT[:, jq, 14:15], -1.0, ALU.mult, ALU.mult,
                [("DST", jq), ("DRS", jq)], [("DNM", jq)])
            act(U, U, AF.Identity, [uk, ("DRS", jq), ("DNM", jq)], [uk], bias=DSTAT[:, jq, 15:16], scale=DSTAT[:, jq, 14:15])
            vtt(U, U, LN1G, ALU.mult, [uk, "LN1G"], [uk])
            vtt(X1[:, jq, :], U, LN1B, ALU.add, [uk, "LN1B"], [("X1", jq)])
        for jq in jqs:
            for hf in range(2):
                bank, bk = nb()
                for q in range(4):
                    kc = hf * 4 + q
                    tr(bank[:, q * 128:(q + 1) * 128], X1[:, jq, kc * 128:(kc + 1) * 128], IDF, [("X1", jq), "CF"], [bk])
                for q in range(4):
                    kc = hf * 4 + q
                    act(H2T[:, kc, jq * 128:(jq + 1) * 128], bank[:, q * 128:(q + 1) * 128], AF.Identity,
                        [bk, "MODP"], [("H2T", jq)], bias=MODP[:, 24 + kc:25 + kc], scale=MODP[:, 32 + kc:33 + kc])

    wgu_i = [0]
    gs_i = [0]

    def phase_E(wt, pool="ffn"):
        own = wt >= 4
        c0 = 0 if own else 384
        N = 512 - c0
        h2k = [("H2T", jq) for jq in (range(4) if own else [3])]
        cx = {}

        def s1(fc):
            s = wgu_i[0] % 4
            wgu_i[0] += 1
            ld(WGU[s][:, 0], WG_S[fc], [("WG_S", fc)], [("WGU", s, 0)])
            if own:
                ld(WGU[s][:, 1], WU_S[fc], [("WU_S", fc)], [("WGU", s, 1)])
            pg, bg = nb(pool)
            for kc in range(8):
                mm(pg[:, 0:N], WGU[s][:, 0, kc, :], H2T[:, kc, c0:512], kc == 0, kc == 7, [("WGU", s, 0)] + h2k, [bg])
            if not own:
                vts(HALO[:, fc, :], pg[:, N - 2:N], HM, None, ALU.mult, ALU.bypass, [bg, "SMALL"], [("HALO", fc)])
                return
            cx[fc] = (pg, bg, s)

        def s1b(fc):
            pg, bg, s = cx[fc]
            pu, bu = nb(pool)
            for kc in range(8):
                mm(pu[:, 0:N], WGU[s][:, 1, kc, :], H2T[:, kc, c0:512], kc == 0, kc == 7, [("WGU", s, 1)] + h2k, [bu])
            cx[fc] = (pg, bg, pu, bu)

        def s2(fc):
            pg, bg, pu, bu = cx.pop(fc)
            g = fc % 2
            vcopy(GS[g][:, 2:2 + N], pg[:, 0:N], [bg], [("GS", g)])
            vcopy(US[g], pu[:, 0:N], [bu], [("US", g)])
            vcopy(GS[g][:, 0:2], HALO[:, fc, :], [("HALO", fc)], [("GS", g)])
            vts(CT[g], GS[g][:, 2:2 + N], CW[:, fc, 2:3], CW[:, fc, 3:4], ALU.mult, ALU.add,
                [("GS", g), "SMALL"], [("CT", g)])
            for tap in (1, 0):
                vstt(CT[g], GS[g][:, tap:tap + N], CW[:, fc, tap:tap + 1], CT[g], ALU.mult, ALU.add,
                     [("GS", g), "SMALL", ("CT", g)], [("CT", g)])
            vcopy(HALO[:, fc, :], GS[g][:, N:N + 2], [("GS", g)], [("HALO", fc)])

        def s3a(fc):
            g = fc % 2
            act(SL[g], CT[g], AF.Exp, [("CT", g)], [("SL", g)], scale=-1.0)
            act(SL[g], SL[g], AF.Ln, [("SL", g)], [("SL", g)], bias=1.0)
            act(SL[g], SL[g], AF.Exp, [("SL", g)], [("SL", g)], scale=-1.0)

        def s3b(fc):
            g = fc % 2
            vtt(CT[g], CT[g], SL[g], ALU.mult, [("CT", g), ("SL", g)], [("CT", g)], eng="pool")
            vtt(ACTT[:, fc, :], CT[g], US[g], ALU.mult, [("CT", g), ("US", g)], [("ACTT", fc)], eng="pool")

        if not own:
            for fc in range(NFC):
                s1(fc)
                yield
            return
        for i in range(NFC + 2):
            if 0 <= i - 2 < NFC:
                s3b(i - 2)
            if 0 <= i - 1 < NFC:
                s3a(i - 1)
            if i < NFC:
                s1(i)
            yield
            if i < NFC:
                s1b(i)
                yield
                s2(i)
                yield

    wd_i = [0]

    def phase_F(wt, pool="ffn"):
        groups = [[0, 1], [2, 3]] if pool == "all" else [[0], [1], [2], [3]]
        for grp in groups:
            banks = {jq: [nb(pool), nb(pool)] for jq in grp}
            for fc in range(NFC):
                s = wd_i[0] % 3
                wd_i[0] += 1
                ld(WDR[s], WD_S[fc], [("WD_S", fc)], [("WDR", s)])
                for jq in grp:
                    for hf in range(2):
                        bank, bk = banks[jq][hf]
                        mm(bank[:, :], ACTT[:, fc, jq * 128:(jq + 1) * 128], WDR[s][:, hf * 512:(hf + 1) * 512],
                           fc == 0, fc == NFC - 1, [("ACTT", fc), ("WDR", s)], [bk])
                if fc % 2 == 1:
                    yield
            for jq in grp:
                blk = 4 * wt + jq
                for hf in range(2):
                    bank, bk = banks[jq][hf]
                    vstt(U2[:, hf * 512:(hf + 1) * 512], X1[:, jq, hf * 512:(hf + 1) * 512], ALPHA, bank[:, :],
                         ALU.mult, ALU.add, [("X1", jq), bk], ["U2"])
                for hf in range(2):
                    S.op("dve", lambda e, hf=hf: e.bn_stats(out=STAT[:, 32 + 6 * hf:38 + 6 * hf],
                                                          in_=U2[:, hf * 512:(hf + 1) * 512]), ["U2"], [("BN", hf)])
                S.op("dve", lambda e: e.bn_aggr(out=STAT[:, 44:46], in_=STAT[:, 32:44]), [("BN", 0), ("BN", 1)], ["MV"])
                yield
                act(STAT[:, 46:47], STAT[:, 45:46], AF.Ln, ["MV"], ["RSTD"], bias=LN_EPS)
                act(STAT[:, 46:47], STAT[:, 46:47], AF.Exp, ["RSTD"], ["RSTD"], scale=-0.5)
                yield
                yield
                vts(STAT[:, 47:48], STAT[:, 44:45], STAT[:, 46:47], -1.0, ALU.mult, ALU.mult, ["MV", "RSTD"], ["NMR"])
                vts(U2, U2, STAT[:, 46:47], STAT[:, 47:48], ALU.mult, ALU.add, ["U2", "RSTD", "NMR"], ["U2"])
                vtt(U2, U2, LN2G[:], ALU.mult, ["U2", "LN2G"], ["U2"])
                vtt(U2, U2, LN2B[:], ALU.add, ["U2", "LN2B"], ["U2"])
                r0 = (blk - 16) * 128
                ld(out_d[r0:r0 + 128, :], U2, ["U2"], [("OUT", blk)], q="pool")
                yield
                yield

    def phase_EF(wt, pool="ffn"):
        yield from phase_E(wt, pool)
        if wt >= 4:
            yield from phase_F(wt, pool)

    def run(gen):
        for _ in gen:
            pass

    def merge(ga, gb, na, nb_):
        ia = ib = 0
        da = db = False
        while not (da and db):
            if not da and (db or ia * nb_ <= ib * na):
                try:
                    next(ga)
                    ia += 1
                except StopIteration:
                    da = True
            elif not db:
                try:
                    next(gb)
                    ib += 1
                except StopIteration:
                    db = True

    n_tiles = 8 if debug is None else debug.get("n_tiles", 8)
    for wt in range(min(n_tiles, 3)):
        phase_A(wt)
        phase_B(wt)
    if n_tiles > 3:
        phase_A(3)
        phase_B(3)
        S.barrier()
        run(phase_C(3))
        S.barrier()
        phase_D(3)
        S.barrier()
        if n_tiles <= 4:
            run(phase_EF(3, "all"))
    for wt in range(4, n_tiles):
        S.barrier()
        phase_A(wt)
        phase_B(wt)
        S.barrier()
        if wt == 4:
            merge(phase_C(wt), phase_EF(3), 8 * (2 * wt + 4) + 2, NFC)
        else:
            nC = 8 * (2 * wt + 4) + 2
            nEF = 3 * NFC + 2 + 4 * (NFC // 2 + 5)
            merge(phase_C(wt), phase_EF(wt - 1), nC, nEF)
        S.barrier()
        phase_D(wt)
    if n_tiles > 4:
        S.barrier()
        run(phase_EF(n_tiles - 1, "all"))
    S.barrier()
    if debug is not None and debug.get("dumps"):
        avail = {"KT": (KT[:], [P, 4, SEQ], BF16), "VA": (VA[:], [P, NBLK, 8, 65], BF16), "MODP": (MODP[:], [P, 48], F32),
                 "S2": (S2[:], [P, 2, 128], F32), "HALO": (HALO[:], [P, NFC, 2], F32),
                 "X1": (X1, [P, 4, D], F32), "H2T": (H2T, [P, 8, 512], BF16), "ON": (ON, [P, 4, D], BF16),
                 "QT": (QT, [P, 4, 512], BF16), "OACC": (OACC, [P, 4, 8, 64], F32), "ACTT": (ACTT, [P, NFC, 512], BF16),
                 "STAT": (STAT[:], [P, 64], F32), "LRA": (LRA[:], [33, 512], F32)}
        for nm in debug["dumps"]:
            ap, shp, dt = avail[nm]
            dd = nc.dram_tensor("dbg_" + nm, shp, dt, kind="ExternalOutput").ap()
            ld(dd, ap, (), [("DBG", nm)])
        S.barrier()

    with nc.Block() as block:
        @block.tensor
        def _(e):
            S.replay("pe", e)

        @block.scalar
        def _(e):
            S.replay("act", e)

        @block.vector
        def _(e):
            S.replay("dve", e)

        @block.gpsimd
        def _(e):
            S.replay("pool", e)

        @block.sync
        def _(e):
            S.replay("sp", e)
    return nc


def make_consts():
    cf = np.zeros((P, NCF), np.float32)
    i = np.arange(P)
    cf[:, C_ID:C_ID + 128] = np.eye(P, dtype=np.float32)
    cf[:, C_NEGU:C_NEGU + 128] = -(i[:, None] >= i[None, :]).astype(np.float32)
    cf[:, C_MASKNEG:C_MASKNEG + 128] = NEG * (i[:, None] >= i[None, :]).astype(np.float32)
    cf[:, C_TRIS:C_TRIS + 128] = (-1.0 / 16.0) * (i[:, None] <= i[None, :]).astype(np.float32)
    cf[:, C_SUPS:C_SUPS + 128] = (-1.0 / 16.0) * (i[:, None] > i[None, :]).astype(np.float32)
    cf[:, C_MASK01:C_MASK01 + 128] = (i[None, :] >= i[:, None]).astype(np.float32)
    cf[:, C_ONES:C_ONES + 128] = 1.0
    return cf


_NC_CACHE = {}


def kernel(x, c, w_ada, b_ada, w_in, gla_w_gate, gla_b_gate, sb_norm_g, gla_norm_g, w_out,
           ln1_g, ln1_b, w_ff_gate, w_ff_up, conv_w, conv_b, w_down, ln2_g, ln2_b):
    f = lambda a: np.ascontiguousarray(np.asarray(a, dtype=np.float32))
    x = f(x)
    c = f(c)
    B = x.shape[0]
    if "nc" not in _NC_CACHE:
        _NC_CACHE["nc"] = build_program()
    nc = _NC_CACHE["nc"]
    cf = make_consts()
    gcat = np.concatenate([f(sb_norm_g)[0], f(gla_norm_g)[0]]).reshape(8, P).T
    cw = np.concatenate([f(conv_w)[0], f(conv_b)[0][None, :]], axis=0)
    cw = cw.reshape(4, NFC, P).transpose(2, 1, 0).reshape(P, NFC * 4)
    shared = {
        "cf": cf,
        "w_ada": f(w_ada)[0], "b_ada": f(b_ada)[0][None, :], "w_in": f(w_in)[0],
        "gla_w_gate": f(gla_w_gate)[0], "gla_b_gate": f(gla_b_gate)[0][None, :], "w_out": f(w_out)[0],
        "ln1_g": f(ln1_g)[0][None, :], "ln1_b": f(ln1_b)[0][None, :],
        "w_ff_gate": f(w_ff_gate)[0], "w_ff_up": f(w_ff_up)[0], "w_down": f(w_down)[0],
        "ln2_g": f(ln2_g)[0][None, :], "ln2_b": f(ln2_b)[0][None, :],
    }
    in_maps = []
    for cid in range(8):
        b, g = cid // 2, cid % 2
        small = np.zeros((P, 137), np.float32)
        if g == 0:
            xw = np.zeros((SEQ, D), np.float32)
            xw[2048:] = x[b, 0:2048]
            small[:, 16:32] = 1.0
        else:
            xw = x[b]
            small[:, 0:32] = 1.0
        small[:, 32] = float(g)
        small[:, 33:41] = c[b].reshape(8, P).T
        small[:, 41:49] = gcat
        small[:, 49:137] = cw
        m = dict(shared)
        m["xw"] = np.ascontiguousarray(xw)
        m["small"] = small
        in_maps.append(m)
    res = run_bass_kernel_spmd(nc, in_maps, core_ids=list(range(8)))
    out = np.zeros((B, SEQ, D), np.float32)
    for cid in range(8):
        b, g = cid // 2, cid % 2
        out[b, g * 2048:(g + 1) * 2048] = res.results[cid]["out"]
    return out
```

```python
import numpy as np
import ml_dtypes
import concourse.bass as bass
import concourse.mybir as mybir
from concourse.bass_utils import run_bass_kernel_spmd

F32 = mybir.dt.float32
BF16 = mybir.dt.bfloat16
AF = mybir.ActivationFunctionType
ALU = mybir.AluOpType
AX = mybir.AxisListType

P = 128
D = 1024
SEQ = 4096
NBLK = 32
DFF = 2816
NFC = 22
INW = 3088
ALPHA = float(2.0 ** 0.25)
LN_EPS = 1e-5
RMS_EPS = 1e-6
HALO_BLK = 15
NEG = -30000.0

C_ID, C_NEGU, C_MASKNEG, C_TRIS, C_SUPS, C_MASK01, C_ONES = 0, 128, 256, 384, 512, 640, 768
NCF = 896


def cc_col0(cc):
    return 128 * cc if cc <= 20 else 2576 + 128 * (cc - 21)


class Sched:
    def __init__(self, nc, same_engine_sync=True):
        self.nc = nc
        self.same = same_engine_sync
        self.eng = {}
        for name, h in (("pe", nc.tensor), ("act", nc.scalar), ("dve", nc.vector),
                        ("pool", nc.gpsimd), ("sp", nc.sync)):
            self.eng[name] = dict(h=h, sem=nc.alloc_semaphore("s_" + name), count=0, insts=[], waited={})
        self.bufs = {}
        self.dring = {}
        for q, k in (("sp", 16), ("pool", 8), ("act", 4)):
            self.dring[q] = dict(sems=[dict(sem=nc.alloc_semaphore(f"d_{q}{i}"), count=0) for i in range(k)], i=0)
        self.sems = {}
        for e in self.eng.values():
            self.sems[e["sem"].num] = e["sem"]
        for r in self.dring.values():
            for s in r["sems"]:
                self.sems[s["sem"].num] = s["sem"]

    def _deps(self, ename, reads, writes):
        deps = {}

        def add(tok):
            if tok is None:
                return
            n, v = tok
            if deps.get(n, 0) < v:
                deps[n] = v

        for r in reads:
            b = self.bufs.get(r)
            if b is not None:
                add(b[0])
        for w in writes:
            b = self.bufs.get(w)
            if b is not None:
                add(b[0])
                for n, v in b[1].items():
                    add((n, v))
        e = self.eng[ename]
        out = []
        for n, v in deps.items():
            if n == e["sem"].num and (ename == "pe" or not self.same):
                continue
            if e["waited"].get(n, 0) >= v:
                continue
            e["waited"][n] = v
            out.append((n, v))
        return out

    def _commit(self, tok, reads, writes):
        for w in writes:
            self.bufs[w] = [tok, {}]
        for r in reads:
            b = self.bufs.setdefault(r, [None, {}])
            if b[1].get(tok[0], 0) < tok[1]:
                b[1][tok[0]] = tok[1]

    def op(self, ename, fn, reads=(), writes=()):
        e = self.eng[ename]
        waits = self._deps(ename, reads, writes)
        e["count"] += 1
        tok = (e["sem"].num, e["count"])
        e["insts"].append((waits, fn, (e["sem"].num, 1)))
        self._commit(tok, reads, writes)
        return tok

    def dma(self, q, fn, reads=(), writes=(), after=()):
        e = self.eng[q]
        ring = self.dring[q]
        ent = ring["sems"][ring["i"] % len(ring["sems"])]
        ring["i"] += 1
        waits = self._deps(q, reads, writes)
        for (tn, tv) in after:
            if e["waited"].get(tn, 0) < tv:
                e["waited"][tn] = tv
                waits.append((tn, tv))
        n = ent["sem"].num
        if ent["count"] > 0 and e["waited"].get(n, 0) < ent["count"]:
            e["waited"][n] = ent["count"]
            waits.append((n, ent["count"]))
        ent["count"] += 16
        tok = (n, ent["count"])
        e["insts"].append((waits, fn, (n, 16)))
        self._commit(tok, reads, writes)
        return tok

    def barrier(self):
        targets = {}
        for e in self.eng.values():
            if e["count"] > 0:
                targets[e["sem"].num] = e["count"]
        for r in self.dring.values():
            for s in r["sems"]:
                if s["count"] > 0:
                    targets[s["sem"].num] = s["count"]
        for e in self.eng.values():
            waits = []
            for n, v in targets.items():
                if n == e["sem"].num:
                    continue
                if e["waited"].get(n, 0) >= v:
                    continue
                e["waited"][n] = v
                waits.append((n, v))
            if waits:
                e["insts"].append((waits, None, None))

    def replay(self, ename, h):
        for waits, fn, inc in self.eng[ename]["insts"]:
            for n, v in waits:
                h.wait_ge(self.sems[n], v)
            if fn is not None:
                ins = fn(h)
                ins.then_inc(self.sems[inc[0]], inc[1])


def build_program(debug=None):
    nc = bass.Bass("TRN2", target_bir_lowering=False)
    S = Sched(nc)

    def din(name, shape, dt=F32):
        return nc.dram_tensor(name, list(shape), dt, kind="ExternalInput").ap()

    xw = din("xw", [SEQ, D])
    cf_d = din("cf", [P, NCF])
    small_d = din("small", [P, 32 + 1 + 8 + 8 + 88])
    w_ada = din("w_ada", [D, 6 * D])
    b_ada = din("b_ada", [1, 6 * D])
    w_in = din("w_in", [D, INW])
    gwg = din("gla_w_gate", [16, 256])
    gbg = din("gla_b_gate", [1, 256])
    w_out = din("w_out", [D, D])
    ln1_g = din("ln1_g", [1, D])
    ln1_b = din("ln1_b", [1, D])
    w_gate = din("w_ff_gate", [D, DFF])
    w_up = din("w_ff_up", [D, DFF])
    w_down = din("w_down", [DFF, D])
    ln2_g = din("ln2_g", [1, D])
    ln2_b = din("ln2_b", [1, D])
    out_d = nc.dram_tensor("out", [2048, D], F32, kind="ExternalOutput").ap()

    WIN_S = nc.dram_tensor("win_s", [25, P, 8, 128], BF16).ap()
    WG_S = nc.dram_tensor("wg_s", [NFC, P, 8, 128], BF16).ap()
    WU_S = nc.dram_tensor("wu_s", [NFC, P, 8, 128], BF16).ap()
    WD_S = nc.dram_tensor("wd_s", [NFC, P, D], BF16).ap()
    WOUT_S = nc.dram_tensor("wout_s", [P, 8, D], BF16).ap()

    def sb(name, shape, dt):
        return nc.alloc_sbuf_tensor(name, list(shape), dt)

    KT = sb("KT", [P, 4, SEQ], BF16)
    VA = sb("VA", [P, NBLK, 8, 65], BF16)
    CF = sb("CF", [P, NCF], F32)
    SMALL = sb("SMALL", [P, 137], F32)
    TM = SMALL[:, 0:32]
    HM = SMALL[:, 32:33]
    CCOL = SMALL[:, 33:41]
    GCAT = SMALL[:, 41:49]
    CW = SMALL[:, 49:137].rearrange("p (f k) -> p f k", k=4)
    IDB = sb("IDB", [P, P], BF16)
    NEGUB = sb("NEGUB", [P, P], BF16)
    MASKNEGB = sb("MASKNEGB", [P, P], BF16)
    NEGONESB = sb("NEGONESB", [P, P], BF16)
    MODP = sb("MODP", [P, 48], F32)
    CACT = sb("CACT", [P, 8], F32)
    LN2G = sb("LN2G", [P, D], F32)
    LN2B = sb("LN2B", [P, D], F32)
    S2 = sb("S2", [P, 2, 128], F32)
    S2B = sb("S2B", [P, 2, 128], BF16)
    WGA = sb("WGA", [33, 256], F32)
    LRA = sb("LRA", [33, 512], F32)
    HALO = sb("HALO", [P, NFC, 2], F32)
    STAT = sb("STAT", [P, 64], F32)

    ARENA_BYTES = 125 * 1024
    ARENA = sb("ARENA", [P, ARENA_BYTES // 4], F32)

    class Carver:
        def __init__(self, base=0):
            self.off = base

        def take(self, shape, dt, parts=P):
            esz = 4 if dt == F32 else 2
            n = int(np.prod(shape[1:]))
            nbytes = (n * esz + 31) // 32 * 32
            assert self.off % 4 == 0
            assert self.off + nbytes <= ARENA_BYTES, ("arena overflow", self.off, nbytes)
            w0 = self.off // 4
            ap = ARENA[0:shape[0], w0:w0 + nbytes // 4]
            if dt != F32:
                ap = ap.bitcast(dt)
            ap = ap[:, 0:n]
            self.off += nbytes
            if len(shape) == 3:
                ap = ap.rearrange("p (a b) -> p a b", a=shape[1])
            elif len(shape) == 4:
                ap = ap.rearrange("p (a b c) -> p a b c", a=shape[1], b=shape[2])
            return ap

    PP = [nc.alloc_psum_tensor(f"PP{i}", [P, 1024], F32) for i in range(4)]
    PB = [PP[i // 2][:, (i % 2) * 512:(i % 2 + 1) * 512] for i in range(8)]
    PT = PB[7].bitcast(BF16)
    PTK = ("PB", 7)
    PZ2, PC2 = PP[0], PP[1]
    pools = {"gen": [0, 1, 2, 3, 4, 5, 6], "po": [4, 5], "ffn": [6, 7], "all": [0, 1, 2, 3, 4, 5, 6, 7]}
    pbi = {"gen": 0, "po": 0, "ffn": 0, "all": 0}

    def nb(pool="gen"):
        lst = pools[pool]
        i = lst[pbi[pool] % len(lst)]
        pbi[pool] += 1
        return PB[i], ("PB", i)

    def mm(out, lhsT, rhs, start, stop, reads, writes):
        return S.op("pe", lambda e: e.matmul(out, lhsT=lhsT, rhs=rhs, start=start, stop=stop), reads, writes)

    def tr(out, in_, ident, reads, writes):
        return S.op("pe", lambda e: e.transpose(out=out, in_=in_, identity=ident), reads, writes)

    def act(out, in_, func, reads, writes, bias=0.0, scale=1.0):
        return S.op("act", lambda e: e.activation(out=out, in_=in_, func=func, bias=bias, scale=scale), reads, writes)

    def vts(out, in0, s1, s2, op0, op1, reads, writes, eng="dve"):
        return S.op(eng, lambda e: e.tensor_scalar(out=out, in0=in0, scalar1=s1, scalar2=s2, op0=op0, op1=op1),
                    reads, writes)

    def vtt(out, in0, in1, op, reads, writes, eng="dve"):
        return S.op(eng, lambda e: e.tensor_tensor(out=out, in0=in0, in1=in1, op=op), reads, writes)

    def vstt(out, in0, scalar, in1, op0, op1, reads, writes, eng="dve"):
        return S.op(eng, lambda e: e.scalar_tensor_tensor(out=out, in0=in0, scalar=scalar, in1=in1, op0=op0, op1=op1),
                    reads, writes)

    def vcopy(out, in_, reads, writes, eng="dve"):
        return S.op(eng, lambda e: e.tensor_copy(out=out, in_=in_), reads, writes)

    def vmemset(ap, val, writes, eng="dve"):
        return S.op(eng, lambda e: e.memset(ap, val), (), writes)

    def ld(out, in_, reads, writes, q="sp", after=()):
        return S.dma(q, lambda e: e.dma_start(out=out, in_=in_), reads, writes, after)

    ld(CF[:], cf_d, (), ["CF"])
    ld(SMALL[:], small_d, (), ["SMALL"])
    vcopy(IDB[:], CF[:, C_ID:C_ID + 128], ["CF"], ["IDB"])
    vcopy(NEGUB[:], CF[:, C_NEGU:C_NEGU + 128], ["CF"], ["NEGUB"])
    vcopy(MASKNEGB[:], CF[:, C_MASKNEG:C_MASKNEG + 128], ["CF"], ["MASKNEGB"])
    vts(NEGONESB[:], CF[:, C_ONES:C_ONES + 128], -1.0, None, ALU.mult, ALU.bypass, ["CF"], ["NEGONESB"])
    IDF = CF[:, C_ID:C_ID + 128]
    TRIS = CF[:, C_TRIS:C_TRIS + 128]
    SUPS = CF[:, C_SUPS:C_SUPS + 128]
    MASK01 = CF[:, C_MASK01:C_MASK01 + 128]
    ONE11 = CF[0:1, 0:1]
    ONESROW = CF[0:1, C_ONES:C_ONES + 128]

    def cast_win(cc, after=()):
        c0 = cc_col0(cc)
        ncol = 16 if cc == 20 else 128
        src = w_in[:, c0:c0 + ncol].rearrange("(kc p) n -> p kc n", p=P)
        ld(WIN_S[cc, :, :, 0:ncol], src, (), [("WIN_S", cc)], q="pool", after=after)

    Q_CCS = [0, 1, 2, 3, 12, 13, 21, 22, 23, 24]
    for cc in range(25):
        if cc not in Q_CCS:
            cast_win(cc)

    deferred = []
    for cc in Q_CCS:
        deferred.append(lambda after, cc=cc: cast_win(cc, after))
    ld(LN2G[:], ln2_g.partition_broadcast(P), (), ["LN2G"])
    ld(LN2B[:], ln2_b.partition_broadcast(P), (), ["LN2B"])
    vmemset(WGA[:], 0.0, ["WGA"])
    ld(WGA[0:16, :], gwg, (), ["WGA"])
    ld(WGA[32:33, :], gbg, (), ["WGA"])
    vmemset(LRA[:], 0.0, ["LRA"])
    vmemset(LRA[32:33, :], 1.0, ["LRA"])
    vmemset(S2[:], 0.0, ["S2"])
    vmemset(S2B[:], 0.0, ["S2B"])
    vmemset(HALO[:], 0.0, ["HALO"])
    vcopy(VA[:, :, :, 64], TM.unsqueeze(2).to_broadcast([P, NBLK, 8]), ["SMALL"], ["VA1"])

    cv = Carver(0)
    G2B = cv.take([P, D], F32)
    WO32 = [cv.take([P, D], F32) for _ in range(2)]
    WOB = [cv.take([P, D], BF16) for _ in range(2)]
    assert cv.off <= 24 * 1024
    cv.off = 24 * 1024
    WA = [cv.take([P, 8, 512], F32) for _ in range(2)]
    MODROW = cv.take([1, 6 * D], F32)
    BADA = cv.take([1, 6 * D], F32)
    G1B = cv.take([P, D], F32)
    act(CACT[:], CCOL, AF.Silu, ["SMALL"], ["CACT"])
    ld(BADA, b_ada, (), ["BADA"])
    for pc in range(12):
        s = pc % 2
        src = w_ada[:, pc * 512:(pc + 1) * 512].rearrange("(kc p) n -> p kc n", p=P)
        ld(WA[s], src, (), [("WA", s)])
        bank, bk = nb()
        for kc in range(8):
            mm(bank[0:1, :], CACT[:, kc:kc + 1], WA[s][:, kc, :], kc == 0, False, [("WA", s), "CACT"], [bk])
        mm(bank[0:1, :], ONE11, BADA[0:1, pc * 512:(pc + 1) * 512], False, True, ["BADA", "CF"], [bk])
        vcopy(MODROW[0:1, pc * 512:(pc + 1) * 512], bank[0:1, :], [bk], ["MODROW"])
    bank, bk = nb()
    for j in range(48):
        mm(bank[:, j:j + 1], MODROW[0:1, j * 128:(j + 1) * 128], ONE11, True, True, ["MODROW", "CF"], [bk])
    vcopy(MODP[:], bank[:, 0:48], [bk], ["MODP"])
    vts(MODP[:, 8:16], MODP[:, 8:16], 1.0, None, ALU.add, ALU.bypass, ["MODP"], ["MODP"])
    vts(MODP[:, 32:40], MODP[:, 32:40], 1.0, None, ALU.add, ALU.bypass, ["MODP"], ["MODP"])
    for (dst, dk, base) in ((G1B, "G1B", 2 * D), (G2B, "G2B", 5 * D)):
        for hf in range(2):
            bank, bk = nb()
            mm(bank[:, :], ONESROW, MODROW[0:1, base + hf * 512: base + (hf + 1) * 512], True, True,
               ["MODROW", "CF"], [bk])
            vts(dst[:, hf * 512:(hf + 1) * 512], bank[:, :], 1.0, None, ALU.add, ALU.bypass, [bk], [dk])
    for kc in range(8):
        s = kc % 2
        ld(WO32[s], w_out[kc * 128:(kc + 1) * 128, :], (), [("WO32", s)])
        vstt(WOB[s], WO32[s], GCAT[:, kc:kc + 1], G1B, ALU.mult, ALU.mult,
             [("WO32", s), "SMALL", "G1B"], [("WOB", s)])
        ld(WOUT_S[:, kc, :], WOB[s], [("WOB", s)], ["WOUT_S"])
    for fc in range(NFC):
        def _wd(after, fc=fc):
            s = fc % 2
            ld(WO32[s], w_down[fc * 128:(fc + 1) * 128, :], (), [("WO32", s)], q="pool", after=after)
            vtt(WOB[s], WO32[s], G2B, ALU.mult, [("WO32", s), "G2B"], [("WOB", s)], eng="pool")
            ld(WD_S[fc], WOB[s], [("WOB", s)], [("WD_S", fc)], q="pool")
        deferred.append(_wd)
    for fc in range(NFC):
        def _cg(after, fc=fc):
            src = w_gate[:, fc * 128:(fc + 1) * 128].rearrange("(kc p) n -> p kc n", p=P)
            ld(WG_S[fc], src, (), [("WG_S", fc)], q="pool", after=after)
        def _cu(after, fc=fc):
            src = w_up[:, fc * 128:(fc + 1) * 128].rearrange("(kc p) n -> p kc n", p=P)
            ld(WU_S[fc], src, (), [("WU_S", fc)], q="pool", after=after)
        deferred.append(_cg)
        deferred.append(_cu)

    S.barrier()

    cv = Carver(0)
    X1 = cv.take([P, 4, D], F32)
    H2T = cv.take([P, 8, 512], BF16)
    QT = cv.take([P, 8, 512], BF16)
    ON = cv.take([P, 4, D], BF16)
    base2 = cv.off
    ES2 = cv.take([P, 2, 512], F32)
    SQ8 = ES2[:, 0, :].rearrange("p (h d) -> p h d", h=8)
    SP2 = [cv.take([P, 2, 512], BF16) for _ in range(2)]
    WT2 = [cv.take([P, 2, 512], BF16) for _ in range(2)]
    OACC = cv.take([P, 4, 8, 64], F32)
    baseEF = cv.off
    WGU = [cv.take([P, 2, 8, 128], BF16) for _ in range(4)]
    GS = [cv.take([P, 520], F32) for _ in range(2)]
    CT = [cv.take([P, 512], F32) for _ in range(2)]
    SL = [cv.take([P, 512], F32) for _ in range(2)]
    US = [cv.take([P, 512], BF16) for _ in range(2)]
    ACTT = cv.take([P, NFC, 512], BF16)
    WDR = [cv.take([P, D], BF16) for _ in range(3)]
    U2 = cv.take([P, D], F32)
    endEF = cv.off
    cv.off = baseEF
    XB2s = [cv.take([P, D], F32) for _ in range(4)]
    OTs = [cv.take([P, 8, 128], BF16) for _ in range(4)]
    Us = [cv.take([P, D], F32) for _ in range(4)]
    DSTAT = cv.take([P, 4, 16], F32)
    WOUTT = cv.take([P, 8, D], BF16)
    LN1G = cv.take([P, D], F32)
    LN1B = cv.take([P, D], F32)
    endD = cv.off
    cv.off = base2
    XBs = [cv.take([P, D], F32) for _ in range(4)]
    HT = cv.take([P, 8, 512], BF16)
    RA = [cv.take([P, 8, 128], BF16) for _ in range(6)]
    RB = [cv.take([P, 2, 8, 128], BF16) for _ in range(3)]
    GQT = cv.take([P, 2, 512], F32)
    GKT = cv.take([P, 2, 512], F32)
    GK = cv.take([P, 4, 256], F32)
    GV = cv.take([P, 4, 512], BF16)
    SR = cv.take([P, 4, 512], F32)
    EU = cv.take([P, 256], F32)
    LL = cv.take([P, 256], F32)
    EB = cv.take([P, 2, 128], F32)
    ENB = cv.take([P, 2, 128], F32)
    ED = cv.take([P, 256], F32)
    QD = cv.take([P, 2, 128], BF16)
    KI = cv.take([P, 2, 128], BF16)
    KE = cv.take([P, 256], BF16)
    AM = cv.take([P, 2, 2, 128], BF16)
    OC = cv.take([P, 2, 2, 128], F32)
    SQ = cv.take([P, 2, 2, 128], F32)
    endAB = cv.off

    ra_i = [0]
    rb_i = [0]
    es_i = [0]

    def phase_A(wt):
        own = wt >= 4
        halo = wt == 3
        for j in range(4):
            blk = 4 * wt + j
            XB = XBs[j]
            xk = ("XB", j)
            ld(XB, xw[blk * 128:(blk + 1) * 128, :], (), [xk])
            for hf in range(2):
                bank, bk = nb()
                for q in range(4):
                    kc = hf * 4 + q
                    tr(bank[:, q * 128:(q + 1) * 128], XB[:, kc * 128:(kc + 1) * 128], IDF, [xk, "CF"], [bk])
                for q in range(4):
                    kc = hf * 4 + q
                    vts(HT[:, kc, j * 128:(j + 1) * 128], bank[:, q * 128:(q + 1) * 128],
                        MODP[:, 8 + kc:9 + kc], MODP[:, kc:kc + 1], ALU.mult, ALU.add,
                        [bk, "MODP"], [("HT", j)])
        htk = [("HT", j) for j in range(4)]

        def fm_job(cc, ncol, c0, c1, evac):
            s = ra_i[0] % 6
            ra_i[0] += 1
            ld(RA[s][:, :, 0:ncol], WIN_S[cc, :, :, 0:ncol], [("WIN_S", cc)], [("RA", s)])
            bank, bk = nb()
            for kc in range(8):
                mm(bank[0:ncol, c0:c1], RA[s][:, kc, 0:ncol], HT[:, kc, c0:c1], kc == 0, kc == 7,
                   [("RA", s)] + htk, [bk])
            evac(bank, bk)

        for gi in range(4):
            fm_job(4 + gi, 128, 0, 512,
                   lambda bank, bk, gi=gi: vcopy(KT[:, gi, wt * 512:(wt + 1) * 512], bank[:, :], [bk], [("KT", gi, wt)]))
        for g2 in range(2):
            fm_job(14 + g2, 128, 0, 512,
                   lambda bank, bk, g2=g2: vcopy(GKT[:, g2, :], bank[:, :], [bk], ["GKT"]))
        fm_job(20, 16, 0, 512, lambda bank, bk: vcopy(LRA[0:16, :], bank[0:16, :], [bk], ["LRA"]))
        if own or halo:
            q0 = 0 if own else 384
            vmemset(QT, 0.0, ["QT"])
            for gi in range(4):
                def evq(bank, bk, gi=gi):
                    for par in range(2):
                        r0 = par * 64
                        vts(QT[r0:r0 + 64, 2 * gi + par, q0:512], bank[r0:r0 + 64, q0:512], 0.125, None, ALU.mult,
                            ALU.bypass, [bk], ["QT"])
                fm_job(gi, 128, q0, 512, evq)
            for g2 in range(2):
                fm_job(12 + g2, 128, q0, 512,
                       lambda bank, bk, g2=g2: vcopy(GQT[:, g2, q0:512], bank[:, q0:512], [bk], ["GQT"]))

        def tm_job(cc, blocks, evac):
            s = rb_i[0] % 3
            rb_i[0] += 1
            ld(RB[s], WIN_S[cc:cc + 2].rearrange("c p k n -> p c k n"), [("WIN_S", cc), ("WIN_S", cc + 1)],
               [("RB", s)])
            for j in blocks:
                bank, bk = nb()
                for kc in range(8):
                    mm(bank[:, 0:256], HT[:, kc, j * 128:(j + 1) * 128], RB[s][:, :, kc, :], kc == 0, kc == 7,
                       [("RB", s), ("HT", j)], [bk])
                evac(bank, bk, j)

        allb = range(4)
        for pi in range(2):
            def ev(bank, bk, j, pi=pi):
                blk = 4 * wt + j
                vts(VA[:, blk, 4 * pi:4 * pi + 4, 0:64], bank[:, 0:256].rearrange("p (h d) -> p h d", h=4),
                    TM[:, blk:blk + 1], None, ALU.mult, ALU.bypass, [bk, "SMALL"], [("VA", blk)])
            tm_job(8 + 2 * pi, allb, ev)
        tm_job(14, allb, lambda bank, bk, j: vcopy(GK[:, j, :], bank[:, 0:256], [bk], [("GK", j)]))
        for pi in range(2):
            def ev(bank, bk, j, pi=pi):
                blk = 4 * wt + j
                vts(GV[:, j, pi * 256:(pi + 1) * 256], bank[:, 0:256], TM[:, blk:blk + 1], None, ALU.mult,
                    ALU.bypass, [bk, "SMALL"], [("GV", j)])
            tm_job(16 + 2 * pi, allb, ev)
        if own or halo:
            qblocks = range(4) if own else [3]
            for pi in range(2):
                def ev(bank, bk, j, pi=pi):
                    act(SR[:, j, pi * 256:(pi + 1) * 256], bank[:, 0:256], AF.Silu, [bk], [("SR", j)])
                tm_job(21 + 2 * pi, qblocks, ev)

    def phase_B(wt):
        for j in range(4):
            blk = 4 * wt + j
            full = blk >= HALO_BLK
            jc = slice(j * 128, (j + 1) * 128)
            bu, bku = nb()
            mm(bu[:, 0:256], LRA[0:33, jc], WGA[0:33, :], True, True, ["LRA", "WGA"], [bku])
            act(EU, bu[:, 0:256], AF.Exp, [bku], ["EU"], scale=-1.0)
            act(LL, EU, AF.Ln, ["EU"], ["LL"], bias=1.0)
            bb, bkb = nb()
            for g2 in range(2):
                mm(bb[:, g2 * 128:(g2 + 1) * 128], LL[:, g2 * 128:(g2 + 1) * 128], TRIS, True, True, ["LL", "CF"], [bkb])
            bd, bkd = nb()
            mm(bd[:, 0:256], SUPS, LL, True, True, ["LL", "CF"], [bkd])
            bb3 = bb[:, 0:256].rearrange("p (g t) -> p g t", g=2)
            if full:
                act(EB, bb3, AF.Exp, [bkb], ["EB"], bias=float(np.log(0.125)))
            act(ENB, bb3, AF.Exp, [bkb], ["ENB"], scale=-1.0)
            act(STAT[:, 0:2], bb3[:, :, 127], AF.Exp, [bkb], ["DEC"])
            act(ED, bd[:, 0:256], AF.Exp, [bkd], ["ED"])
            vtt(KI, GKT[:, :, jc], ENB, ALU.mult, ["GKT", "ENB"], ["KI"])
            vtt(KE, GK[:, j, :], ED, ALU.mult, [("GK", j), "ED"], ["KE"])
            if full:
                jq = j if wt >= 4 else 3
                vtt(QD, GQT[:, :, jc], EB, ALU.mult, ["GQT", "EB"], ["QD"])
                bas = [nb(), nb()]
                for h in range(4):
                    g2, par = h // 2, h % 2
                    r0 = par * 64
                    ba, bka = bas[par]
                    mm(ba[:, g2 * 128:(g2 + 1) * 128], KI[r0:r0 + 64, g2, :], QD[r0:r0 + 64, g2, :], True, True,
                       ["KI", "QD"], [bka])
                for par in range(2):
                    ba, bka = bas[par]
                    vtt(AM[:, par], ba[:, 0:256].rearrange("p (g t) -> p g t", g=2),
                        MASK01.unsqueeze(1).to_broadcast([P, 2, 128]), ALU.mult, [bka, "CF"], [("AM", par)])
                bos = [nb(), nb()]
                for h in range(4):
                    g2, par = h // 2, h % 2
                    r0 = par * 64
                    bo, bko = bos[par]
                    mm(bo[:, g2 * 128:(g2 + 1) * 128], AM[:, par, g2, :], GV[:, j, h * 128:(h + 1) * 128], True, False,
                       [("AM", par), ("GV", j)], [bko])
                    mm(bo[:, g2 * 128:(g2 + 1) * 128], QD[r0:r0 + 64, g2, :], S2B[r0:r0 + 64, g2, :], False, True,
                       ["QD", "S2B"], [bko])
                for par in range(2):
                    bo, bko = bos[par]
                    vcopy(OC[:, par], bo[:, 0:256].rearrange("p (g t) -> p g t", g=2), [bko], ["OC"])
                vtt(SQ, OC, OC, ALU.mult, ["OC"], ["SQ"])
                S.op("dve", lambda e: e.reduce_sum(out=STAT[:, 8:12], in_=SQ.rearrange("p r g t -> p (r g) t"), axis=AX.X),
                     ["SQ"], ["SS4"])
                vts(STAT[:, 8:12], STAT[:, 8:12], 1.0 / 128.0, RMS_EPS, ALU.mult, ALU.add, ["SS4"], ["SS4"])
                act(STAT[:, 8:12], STAT[:, 8:12], AF.Ln, ["SS4"], ["SS4"])
                act(STAT[:, 8:12], STAT[:, 8:12], AF.Exp, ["SS4"], ["SS4"], scale=-0.5)
                OC3 = OC.rearrange("p r g t -> p (r g) t")
                vtt(OC3, OC3, STAT[:, 8:12].unsqueeze(2).to_broadcast([P, 4, 128]), ALU.mult, ["OC", "SS4"], ["OC"])
                for par in range(2):
                    vtt(ON[:, jq, 512:1024].rearrange("p (g r t) -> p r g t", g=2, r=2)[:, par], OC[:, par],
                        SR[:, j, :].rearrange("p (g r t) -> p r g t", g=2, r=2)[:, par], ALU.mult,
                        ["OC", ("SR", j)], [("ON", jq, 1)])
            bk_, bkk = nb()
            for h in range(4):
                g2 = h // 2
                mm(bk_[:, h * 128:(h + 1) * 128], KE[:, g2 * 128:(g2 + 1) * 128], GV[:, j, h * 128:(h + 1) * 128],
                   True, True, ["KE", ("GV", j)], [bkk])
            for h in range(4):
                g2, r0 = h // 2, (h % 2) * 64
                vstt(S2[r0:r0 + 64, g2, :], S2[r0:r0 + 64, g2, :], STAT[r0:r0 + 64, g2:g2 + 1],
                     bk_[r0:r0 + 64, h * 128:(h + 1) * 128], ALU.mult, ALU.add, ["S2", "DEC", bkk], ["S2"])
            tok = vcopy(S2B[:], S2[:], ["S2"], ["S2B"])
            for _ in range(6):
                if deferred:
                    deferred.pop(0)([tok])

    def phase_C(wt):
        own = wt >= 4
        qb_first = 4 * wt if own else HALO_BLK
        qb_last = 4 * wt + 3
        units = []
        for h in range(8):
            kb = 0
            while kb <= qb_last:
                if kb + 1 < qb_first:
                    units.append((h, [kb, kb + 1]))
                    kb += 2
                else:
                    units.append((h, [kb]))
                    kb += 1
        ctx = {}

        def prep(n):
            h, kbs = units[n]
            gi = h // 2
            qa = max(kbs[0], qb_first)
            lc0 = (qa - 4 * wt) * 128
            N = 512 - lc0
            c = dict(h=h, kbs=kbs, gi=gi, qa=qa, lc0=lc0, N=N, nq=N // 128, diag=kbs[0] >= qb_first,
                     kts=[KT[:, gi, kb * 128:(kb + 1) * 128] for kb in kbs], qt=QT[:, h, lc0:512],
                     rds=[[("KT", gi, kb // 4), "QT"] for kb in kbs], s=n % 2, nb=len(kbs))
            ctx[n] = c
            return c

        def st1(n):
            c = prep(n)
            N, nbk = c["N"], c["nb"]
            for bi in range(nbk):
                mm(PZ2[:, bi * 512:bi * 512 + N], c["kts"][bi], c["qt"], True, not c["diag"], c["rds"][bi], [("PB", bi)])
                if c["diag"]:
                    mm(PZ2[:, 0:128], IDB[:], MASKNEGB[:], False, True, ["IDB", "MASKNEGB"], [("PB", bi)])
            pk = [("PB", bi) for bi in range(nbk)]
            pzv = PZ2[:, :].rearrange("p (b n) -> p b n", b=2)[:, 0:nbk, 0:N]
            act(ES2[:, 0:nbk, 0:N], pzv, AF.Exp, pk, ["ES2"])
            act(SP2[c["s"]][:, 0:nbk, 0:N], ES2[:, 0:nbk, 0:N], AF.Ln, ["ES2"], [("SP2", c["s"])], bias=1.0)

        def st2(n):
            c = ctx[n]
            N, nbk, s = c["N"], c["nb"], c["s"]
            for bi in range(nbk):
                bk = ("PB", 2 + bi)
                dst = PC2[:, bi * 512:bi * 512 + N]
                mm(dst, c["kts"][bi], c["qt"], True, False, c["rds"][bi], [bk])
                if c["diag"]:
                    mm(PC2[:, 0:128], IDB[:], MASKNEGB[:], False, False, ["IDB", "MASKNEGB"], [bk])
                last = not (nbk == 2 and bi == 0)
                mm(dst, NEGUB[:], SP2[s][:, bi, 0:N], False, last, ["NEGUB", ("SP2", s)], [bk])
                if nbk == 2 and bi == 0:
                    mm(dst, NEGONESB[:], SP2[s][:, 1, 0:N], False, True, ["NEGONESB", ("SP2", s)], [bk])
            pk = [("PB", 2 + bi) for bi in range(nbk)]
            pcv = PC2[:, :].rearrange("p (b n) -> p b n", b=2)[:, 0:nbk, 0:N]
            act(WT2[s][:, 0:nbk, 0:N], pcv, AF.Exp, pk, [("WT2", s)])

        def st3(n):
            c = ctx.pop(n)
            h, kbs, nq, s, nbk = c["h"], c["kbs"], c["nq"], c["s"], c["nb"]
            po, bo = nb("po")
            for qi in range(nq):
                for bi in range(nbk):
                    mm(po[:, qi * 65:(qi + 1) * 65], WT2[s][:, bi, qi * 128:(qi + 1) * 128], VA[:, kbs[bi], h, :],
                       bi == 0, bi == nbk - 1, [("WT2", s), ("VA", kbs[bi]), "VA1"], [bo])
            jq0 = c["qa"] - 4 * wt
            po3 = po[:, 0:nq * 65].rearrange("p (q e) -> p q e", e=65)
            if kbs[0] == 0:
                vcopy(OACC[:, jq0:jq0 + nq, h, :], po3[:, :, 0:64], [bo], [("OACC", h)])
            else:
                F = STAT[:, 16:16 + nq]
                vts(F, po3[:, :, 64], -1.0, 1.0, ALU.mult, ALU.add, [bo], ["F"])
                if nq == 1:
                    vstt(OACC[:, jq0, h, :], OACC[:, jq0, h, :], STAT[:, 16:17],
                         po3[:, 0, 0:64], ALU.mult, ALU.add, [("OACC", h), "F", bo], [("OACC", h)])
                else:
                    oa = OACC[:, jq0:jq0 + nq, h, :]
                    vtt(oa, oa, F.unsqueeze(2).to_broadcast([P, nq, 64]), ALU.mult, [("OACC", h), "F"], [("OACC", h)])
                    vtt(oa, oa, po3[:, :, 0:64], ALU.add, [("OACC", h), bo], [("OACC", h)])

        nblk = len(units)
        for t in range(nblk + 2):
            if t < nblk:
                st1(t)
            if 0 <= t - 1 < nblk:
                st2(t - 1)
            if 0 <= t - 2 < nblk:
                st3(t - 2)
            yield
        jqs = range(4) if own else [3]
        for jq in jqs:
            vtt(SQ8, OACC[:, jq], OACC[:, jq], ALU.mult, [("OACC", h) for h in range(8)], ["ES2"])
            S.op("dve", lambda e: e.reduce_sum(out=STAT[:, 24:32], in_=SQ8, axis=AX.X), ["ES2"], ["SS8"])
            vts(STAT[:, 24:32], STAT[:, 24:32], 1.0 / 64.0, RMS_EPS, ALU.mult, ALU.add, ["SS8"], ["SS8"])
            act(STAT[:, 24:32], STAT[:, 24:32], AF.Ln, ["SS8"], ["SS8"])
            act(STAT[:, 24:32], STAT[:, 24:32], AF.Exp, ["SS8"], ["SS8"], scale=-0.5)
            vtt(ON[:, jq, 0:512].rearrange("p (h d) -> p h d", h=8), OACC[:, jq],
                STAT[:, 24:32].unsqueeze(2).to_broadcast([P, 8, 64]), ALU.mult,
                [("OACC", h) for h in range(8)] + ["SS8"], [("ON", jq, 0)])

    def layer_norm(Ut, uk, G, gk, Bt, bkey, out, okey):
        for hf in range(2):
            S.op("dve", lambda e, hf=hf: e.bn_stats(out=STAT[:, 32 + 6 * hf:38 + 6 * hf], in_=Ut[:, hf * 512:(hf + 1) * 512]),
                 [uk], [("BN", hf)])
        S.op("dve", lambda e: e.bn_aggr(out=STAT[:, 44:46], in_=STAT[:, 32:44]), [("BN", 0), ("BN", 1)], ["MV"])
        act(STAT[:, 46:47], STAT[:, 45:46], AF.Ln, ["MV"], ["RSTD"], bias=LN_EPS)
        act(STAT[:, 46:47], STAT[:, 46:47], AF.Exp, ["RSTD"], ["RSTD"], scale=-0.5)
        vts(STAT[:, 47:48], STAT[:, 44:45], STAT[:, 46:47], -1.0, ALU.mult, ALU.mult, ["MV", "RSTD"], ["NMR"])
        vts(Ut, Ut, STAT[:, 46:47], STAT[:, 47:48], ALU.mult, ALU.add, [uk, "RSTD", "NMR"], [uk])
        vtt(Ut, Ut, G, ALU.mult, [uk, gk], [uk])
        vtt(out, Ut, Bt, ALU.add, [uk, bkey], [okey])

    def phase_D(wt):
        own = wt >= 4
        for kc in range(8):
            ld(WOUTT[:, kc, :], WOUT_S[:, kc, :], ["WOUT_S"], [("WOUTT", kc)])
        ld(LN1G, ln1_g.partition_broadcast(P), (), ["LN1G"])
        ld(LN1B, ln1_b.partition_broadcast(P), (), ["LN1B"])
        jqs = list(range(4)) if own else [3]
        for jq in jqs:
            blk = 4 * wt + jq
            ld(XB2s[jq], xw[blk * 128:(blk + 1) * 128, :], (), [("XB2", jq)])
        for jq in jqs:
            for kc in range(8):
                tr(PT[:, kc * 128:(kc + 1) * 128], ON[:, jq, kc * 128:(kc + 1) * 128], IDB[:],
                   [("ON", jq, 0), ("ON", jq, 1), "IDB"], [PTK])
            vcopy(OTs[jq], PT[:, :].rearrange("p (k t) -> p k t", k=8), [PTK], [("OT", jq)])
        for jq in jqs:
            for hf in range(2):
                py, bky = nb()
                for kc in range(8):
                    mm(py[:, :], OTs[jq][:, kc, :], WOUTT[:, kc, hf * 512:(hf + 1) * 512], kc == 0, kc == 7,
                       [("OT", jq), ("WOUTT", kc)], [bky])
                vstt(Us[jq][:, hf * 512:(hf + 1) * 512], XB2s[jq][:, hf * 512:(hf + 1) * 512], ALPHA, py[:, :],
                     ALU.mult, ALU.add, [("XB2", jq), bky], [("U", jq)])
        for jq in jqs:
            for hf in range(2):
                S.op("dve", lambda e, hf=hf, jq=jq: e.bn_stats(out=DSTAT[:, jq, 6 * hf:6 * hf + 6],
                                                              in_=Us[jq][:, hf * 512:(hf + 1) * 512]),
                     [("U", jq)], [("DST", jq)])
            S.op("dve", lambda e, jq=jq: e.bn_aggr(out=DSTAT[:, jq, 12:14], in_=DSTAT[:, jq, 0:12]),
                 [("DST", jq)], [("DST", jq)])
        for jq in jqs:
            act(DSTAT[:, jq, 14:15], DSTAT[:, jq, 13:14], AF.Ln, [("DST", jq)], [("DRS", jq)], bias=LN_EPS)
        for jq in jqs:
            act(DSTAT[:, jq, 14:15], DSTAT[:, jq, 14:15], AF.Exp, [("DRS", jq)], [("DRS", jq)], scale=-0.5)
        for jq in jqs:
            U = Us[jq]
            uk = ("U", jq)
            vts(DSTAT[:, jq, 15:16], DSTAT[:, jq, 12:13], DSTAT[:, jq, 14:15], -1.0, ALU.mult, ALU.mult,
                [("DST", jq), ("DRS", jq)], [("DNM", jq)])
            act(U, U, AF.Identity, [uk, ("DRS", jq), ("DNM", jq)], [uk], bias=DSTAT[:, jq, 15:16], scale=DSTAT[:, jq, 14:15])
            vtt(U, U, LN1G, ALU.mult, [uk, "LN1G"], [uk])
            vtt(X1[:, jq, :], U, LN1B, ALU.add, [uk, "LN1B"], [("X1", jq)])
        for jq in jqs:
            for hf in range(2):
                bank, bk = nb()
                for q in range(4):
                    kc = hf * 4 + q
                    tr(bank[:, q * 128:(q + 1) * 128], X1[:, jq, kc * 128:(kc + 1) * 128], IDF, [("X1", jq), "CF"], [bk])
                for q in range(4):
                    kc = hf * 4 + q
                    act(H2T[:, kc, jq * 128:(jq + 1) * 128], bank[:, q * 128:(q + 1) * 128], AF.Identity,
                        [bk, "MODP"], [("H2T", jq)], bias=MODP[:, 24 + kc:25 + kc], scale=MODP[:, 32 + kc:33 + kc])

    wgu_i = [0]
    gs_i = [0]

    def phase_E(wt, pool="ffn"):
        own = wt >= 4
        c0 = 0 if own else 384
        N = 512 - c0
        h2k = [("H2T", jq) for jq in (range(4) if own else [3])]
        cx = {}

        def s1(fc):
            s = wgu_i[0] % 4
            wgu_i[0] += 1
            ld(WGU[s][:, 0], WG_S[fc], [("WG_S", fc)], [("WGU", s, 0)])
            if own:
                ld(WGU[s][:, 1], WU_S[fc], [("WU_S", fc)], [("WGU", s, 1)])
            pg, bg = nb(pool)
            for kc in range(8):
                mm(pg[:, 0:N], WGU[s][:, 0, kc, :], H2T[:, kc, c0:512], kc == 0, kc == 7, [("WGU", s, 0)] + h2k, [bg])
            if not own:
                vts(HALO[:, fc, :], pg[:, N - 2:N], HM, None, ALU.mult, ALU.bypass, [bg, "SMALL"], [("HALO", fc)])
                return
            cx[fc] = (pg, bg, s)

        def s1b(fc):
            pg, bg, s = cx[fc]
            pu, bu = nb(pool)
            for kc in range(8):
                mm(pu[:, 0:N], WGU[s][:, 1, kc, :], H2T[:, kc, c0:512], kc == 0, kc == 7, [("WGU", s, 1)] + h2k, [bu])
            cx[fc] = (pg, bg, pu, bu)

        def s2(fc):
            pg, bg, pu, bu = cx.pop(fc)
            g = fc % 2
            vcopy(GS[g][:, 2:2 + N], pg[:, 0:N], [bg], [("GS", g)])
            vcopy(US[g], pu[:, 0:N], [bu], [("US", g)])
            vcopy(GS[g][:, 0:2], HALO[:, fc, :], [("HALO", fc)], [("GS", g)])
            vts(CT[g], GS[g][:, 2:2 + N], CW[:, fc, 2:3], CW[:, fc, 3:4], ALU.mult, ALU.add,
                [("GS", g), "SMALL"], [("CT", g)])
            for tap in (1, 0):
                vstt(CT[g], GS[g][:, tap:tap + N], CW[:, fc, tap:tap + 1], CT[g], ALU.mult, ALU.add,
                     [("GS", g), "SMALL", ("CT", g)], [("CT", g)])
            vcopy(HALO[:, fc, :], GS[g][:, N:N + 2], [("GS", g)], [("HALO", fc)])

        def s3a(fc):
            g = fc % 2
            act(SL[g], CT[g], AF.Exp, [("CT", g)], [("SL", g)], scale=-1.0)
            act(SL[g], SL[g], AF.Ln, [("SL", g)], [("SL", g)], bias=1.0)
            act(SL[g], SL[g], AF.Exp, [("SL", g)], [("SL", g)], scale=-1.0)

        def s3b(fc):
            g = fc % 2
            vtt(CT[g], CT[g], SL[g], ALU.mult, [("CT", g), ("SL", g)], [("CT", g)], eng="pool")
            vtt(ACTT[:, fc, :], CT[g], US[g], ALU.mult, [("CT", g), ("US", g)], [("ACTT", fc)], eng="pool")

        if not own:
            for fc in range(NFC):
                s1(fc)
                yield
            return
        for i in range(NFC + 2):
            if 0 <= i - 2 < NFC:
                s3b(i - 2)
            if 0 <= i - 1 < NFC:
                s3a(i - 1)
            if i < NFC:
                s1(i)
            yield
            if i < NFC:
                s1b(i)
                yield
                s2(i)
                yield

    wd_i = [0]

    def phase_F(wt, pool="ffn"):
        groups = [[0, 1], [2, 3]] if pool == "all" else [[0], [1], [2], [3]]
        for grp in groups:
            banks = {jq: [nb(pool), nb(pool)] for jq in grp}
            for fc in range(NFC):
                s = wd_i[0] % 3
                wd_i[0] += 1
                ld(WDR[s], WD_S[fc], [("WD_S", fc)], [("WDR", s)])
                for jq in grp:
                    for hf in range(2):
                        bank, bk = banks[jq][hf]
                        mm(bank[:, :], ACTT[:, fc, jq * 128:(jq + 1) * 128], WDR[s][:, hf * 512:(hf + 1) * 512],
                           fc == 0, fc == NFC - 1, [("ACTT", fc), ("WDR", s)], [bk])
                if fc % 2 == 1:
                    yield
            for jq in grp:
                blk = 4 * wt + jq
                for hf in range(2):
                    bank, bk = banks[jq][hf]
                    vstt(U2[:, hf * 512:(hf + 1) * 512], X1[:, jq, hf * 512:(hf + 1) * 512], ALPHA, bank[:, :],
                         ALU.mult, ALU.add, [("X1", jq), bk], ["U2"])
                for hf in range(2):
                    S.op("dve", lambda e, hf=hf: e.bn_stats(out=STAT[:, 32 + 6 * hf:38 + 6 * hf],
                                                          in_=U2[:, hf * 512:(hf + 1) * 512]), ["U2"], [("BN", hf)])
                S.op("dve", lambda e: e.bn_aggr(out=STAT[:, 44:46], in_=STAT[:, 32:44]), [("BN", 0), ("BN", 1)], ["MV"])
                yield
                act(STAT[:, 46:47], STAT[:, 45:46], AF.Ln, ["MV"], ["RSTD"], bias=LN_EPS)
                act(STAT[:, 46:47], STAT[:, 46:47], AF.Exp, ["RSTD"], ["RSTD"], scale=-0.5)
                yield
                yield
                vts(STAT[:, 47:48], STAT[:, 44:45], STAT[:, 46:47], -1.0, ALU.mult, ALU.mult, ["MV", "RSTD"], ["NMR"])
                vts(U2, U2, STAT[:, 46:47], STAT[:, 47:48], ALU.mult, ALU.add, ["U2", "RSTD", "NMR"], ["U2"])
                vtt(U2, U2, LN2G[:], ALU.mult, ["U2", "LN2G"], ["U2"])
                vtt(U2, U2, LN2B[:], ALU.add, ["U2", "LN2B"], ["U2"])
                r0 = (blk - 16) * 128
                ld(out_d[r0:r0 + 128, :], U2, ["U2"], [("OUT", blk)], q="pool")
                yield
                yield

    def phase_EF(wt, pool="ffn"):
        yield from phase_E(wt, pool)
        if wt >= 4:
            yield from phase_F(wt, pool)

    def run(gen):
        for _ in gen:
            pass

    def merge(ga, gb, na, nb_):
        ia = ib = 0
        da = db = False
        while not (da and db):
            if not da and (db or ia * nb_ <= ib * na):
                try:
                    next(ga)
                    ia += 1
                except StopIteration:
                    da = True
            elif not db:
                try:
                    next(gb)
                    ib += 1
                except StopIteration:
                    db = True

    n_tiles = 8 if debug is None else debug.get("n_tiles", 8)
    for wt in range(min(n_tiles, 3)):
        phase_A(wt)
        phase_B(wt)
    if n_tiles > 3:
        phase_A(3)
        phase_B(3)
        S.barrier()
        run(phase_C(3))
        S.barrier()
        phase_D(3)
        S.barrier()
        if n_tiles <= 4:
            run(phase_EF(3, "all"))
    for wt in range(4, n_tiles):
        S.barrier()
        phase_A(wt)
        phase_B(wt)
        S.barrier()
        if wt == 4:
            merge(phase_C(wt), phase_EF(3), 8 * (2 * wt + 4) + 2, NFC)
        else:
            nC = 8 * (2 * wt + 4) + 2
            nEF = 3 * NFC + 2 + 4 * (NFC // 2 + 5)
            merge(phase_C(wt), phase_EF(wt - 1), nC, nEF)
        S.barrier()
        phase_D(wt)
    if n_tiles > 4:
        S.barrier()
        run(phase_EF(n_tiles - 1, "all"))
    S.barrier()
    if debug is not None and debug.get("dumps"):
        avail = {"KT": (KT[:], [P, 4, SEQ], BF16), "VA": (VA[:], [P, NBLK, 8, 65], BF16), "MODP": (MODP[:], [P, 48], F32),
                 "S2": (S2[:], [P, 2, 128], F32), "HALO": (HALO[:], [P, NFC, 2], F32),
                 "X1": (X1, [P, 4, D], F32), "H2T": (H2T, [P, 8, 512], BF16), "ON": (ON, [P, 4, D], BF16),
                 "QT": (QT, [P, 4, 512], BF16), "OACC": (OACC, [P, 4, 8, 64], F32), "ACTT": (ACTT, [P, NFC, 512], BF16),
                 "STAT": (STAT[:], [P, 64], F32), "LRA": (LRA[:], [33, 512], F32)}
        for nm in debug["dumps"]:
            ap, shp, dt = avail[nm]
            dd = nc.dram_tensor("dbg_" + nm, shp, dt, kind="ExternalOutput").ap()
            ld(dd, ap, (), [("DBG", nm)])
        S.barrier()

    with nc.Block() as block:
        @block.tensor
        def _(e):
            S.replay("pe", e)

        @block.scalar
        def _(e):
            S.replay("act", e)

        @block.vector
        def _(e):
            S.replay("dve", e)

        @block.gpsimd
        def _(e):
            S.replay("pool", e)

        @block.sync
        def _(e):
            S.replay("sp", e)
    return nc


def make_consts():
    cf = np.zeros((P, NCF), np.float32)
    i = np.arange(P)
    cf[:, C_ID:C_ID + 128] = np.eye(P, dtype=np.float32)
    cf[:, C_NEGU:C_NEGU + 128] = -(i[:, None] >= i[None, :]).astype(np.float32)
    cf[:, C_MASKNEG:C_MASKNEG + 128] = NEG * (i[:, None] >= i[None, :]).astype(np.float32)
    cf[:, C_TRIS:C_TRIS + 128] = (-1.0 / 16.0) * (i[:, None] <= i[None, :]).astype(np.float32)
    cf[:, C_SUPS:C_SUPS + 128] = (-1.0 / 16.0) * (i[:, None] > i[None, :]).astype(np.float32)
    cf[:, C_MASK01:C_MASK01 + 128] = (i[None, :] >= i[:, None]).astype(np.float32)
    cf[:, C_ONES:C_ONES + 128] = 1.0
    return cf


_NC_CACHE = {}


def kernel(x, c, w_ada, b_ada, w_in, gla_w_gate, gla_b_gate, sb_norm_g, gla_norm_g, w_out,
           ln1_g, ln1_b, w_ff_gate, w_ff_up, conv_w, conv_b, w_down, ln2_g, ln2_b):
    f = lambda a: np.ascontiguousarray(np.asarray(a, dtype=np.float32))
    x = f(x)
    c = f(c)
    B = x.shape[0]
    if "nc" not in _NC_CACHE:
        _NC_CACHE["nc"] = build_program()
    nc = _NC_CACHE["nc"]
    cf = make_consts()
    gcat = np.concatenate([f(sb_norm_g)[0], f(gla_norm_g)[0]]).reshape(8, P).T
    cw = np.concatenate([f(conv_w)[0], f(conv_b)[0][None, :]], axis=0)
    cw = cw.reshape(4, NFC, P).transpose(2, 1, 0).reshape(P, NFC * 4)
    shared = {
        "cf": cf,
        "w_ada": f(w_ada)[0], "b_ada": f(b_ada)[0][None, :], "w_in": f(w_in)[0],
        "gla_w_gate": f(gla_w_gate)[0], "gla_b_gate": f(gla_b_gate)[0][None, :], "w_out": f(w_out)[0],
        "ln1_g": f(ln1_g)[0][None, :], "ln1_b": f(ln1_b)[0][None, :],
        "w_ff_gate": f(w_ff_gate)[0], "w_ff_up": f(w_ff_up)[0], "w_down": f(w_down)[0],
        "ln2_g": f(ln2_g)[0][None, :], "ln2_b": f(ln2_b)[0][None, :],
    }
    in_maps = []
    for cid in range(8):
        b, g = cid // 2, cid % 2
        small = np.zeros((P, 137), np.float32)
        if g == 0:
            xw = np.zeros((SEQ, D), np.float32)
            xw[2048:] = x[b, 0:2048]
            small[:, 16:32] = 1.0
        else:
            xw = x[b]
            small[:, 0:32] = 1.0
        small[:, 32] = float(g)
        small[:, 33:41] = c[b].reshape(8, P).T
        small[:, 41:49] = gcat
        small[:, 49:137] = cw
        m = dict(shared)
        m["xw"] = np.ascontiguousarray(xw)
        m["small"] = small
        in_maps.append(m)
    res = run_bass_kernel_spmd(nc, in_maps, core_ids=list(range(8)))
    out = np.zeros((B, SEQ, D), np.float32)
    for cid in range(8):
        b, g = cid // 2, cid % 2
        out[b, g * 2048:(g + 1) * 2048] = res.results[cid]["out"]
    return out
```
